# Optimizing a Trainium2 kernel written in Bass

```python
import math
import jax, jax.numpy as jnp
from jax import lax
import numpy as np

D_MODEL = 1024
BATCH = 8
SEQ = 4096
DEPTH = 4

CHUNK = 64
N_MIXERS = 2
N_SSM_LAYERS = (DEPTH + 1) // 2
N_CONV_LAYERS = DEPTH // 2
SSM_GROUP = 16
SSM_GROUPS = D_MODEL // SSM_GROUP
SSM_STATE = 64
DT_MIN = 0.001
DT_MAX = 0.1
CONV_WIDTH = 3
FFN_HIDDEN = ((8 * D_MODEL // 3 + 255) // 256) * 256
PLE_DIM = 256
EPS = 1e-6

kernel_name = "hybrid_s5_shortconv_streaming_trunk"


def rmsnorm(x, g):
    xf = x.astype(jnp.float32)
    y = xf * lax.rsqrt(jnp.mean(xf * xf, axis=-1, keepdims=True) + EPS)
    return (y * g.astype(jnp.float32)).astype(x.dtype)


def _scan_combine(earlier, later):
    ae_re, ae_im, be_re, be_im = earlier
    al_re, al_im, bl_re, bl_im = later
    a_re = al_re * ae_re - al_im * ae_im
    a_im = al_re * ae_im + al_im * ae_re
    b_re = al_re * be_re - al_im * be_im + bl_re
    b_im = al_re * be_im + al_im * be_re + bl_im
    return (a_re, a_im, b_re, b_im)


def s5_mixer(u, a_re, a_im, log_dt, b_re, b_im, c_re, c_im, d_skip, w_glu):
    bsz, seq, dm = u.shape
    f32 = jnp.float32
    uf = u.astype(f32)
    a_re = a_re.astype(f32); a_im = a_im.astype(f32)
    dt = jnp.exp(log_dt.astype(f32))[:, None]
    mag = jnp.exp(dt * a_re)
    ang = dt * a_im
    abar_re = mag * jnp.cos(ang)
    abar_im = mag * jnp.sin(ang)
    nr = abar_re - 1.0
    ni = abar_im
    den = a_re * a_re + a_im * a_im
    f_re = (nr * a_re + ni * a_im) / den
    f_im = (ni * a_re - nr * a_im) / den
    b_re = b_re.astype(f32); b_im = b_im.astype(f32)
    bb_re = f_re[..., None] * b_re - f_im[..., None] * b_im
    bb_im = f_re[..., None] * b_im + f_im[..., None] * b_re
    c_re = c_re.astype(f32); c_im = c_im.astype(f32)

    n_chunks = seq // CHUNK
    u_chunks = uf.reshape(bsz, n_chunks, CHUNK, SSM_GROUPS, SSM_GROUP).transpose(1, 0, 2, 3, 4)
    a_seq_re = jnp.broadcast_to(abar_re, (1, CHUNK, SSM_GROUPS, SSM_STATE))
    a_seq_im = jnp.broadcast_to(abar_im, (1, CHUNK, SSM_GROUPS, SSM_STATE))

    def step(carry, u_c):
        h_re, h_im = carry
        bu_re = jnp.einsum('bcgk,gpk->bcgp', u_c, bb_re)
        bu_im = jnp.einsum('bcgk,gpk->bcgp', u_c, bb_im)
        acum_re, acum_im, hl_re, hl_im = lax.associative_scan(
            _scan_combine, (a_seq_re, a_seq_im, bu_re, bu_im), axis=1)
        hs_re = hl_re + acum_re * h_re[:, None] - acum_im * h_im[:, None]
        hs_im = hl_im + acum_re * h_im[:, None] + acum_im * h_re[:, None]
        y = (jnp.einsum('bcgp,gkp->bcgk', hs_re, c_re)
             - jnp.einsum('bcgp,gkp->bcgk', hs_im, c_im))
        return (hs_re[:, -1], hs_im[:, -1]), y

    init = (jnp.zeros((bsz, SSM_GROUPS, SSM_STATE), f32),
            jnp.zeros((bsz, SSM_GROUPS, SSM_STATE), f32))
    _, ys = lax.scan(step, init, u_chunks)
    y = ys.transpose(1, 0, 2, 3, 4).reshape(bsz, seq, dm) + d_skip.astype(f32) * uf
    z = jax.nn.gelu(y).astype(w_glu.dtype)
    val, gate = jnp.split(z @ w_glu, 2, axis=-1)
    return (val * jax.nn.sigmoid(gate)).astype(u.dtype)


def shortconv_mixer(u, w_in, w_conv, w_out):
    dm = u.shape[-1]
    b_gate, c_gate, v = jnp.split(u @ w_in, 3, axis=-1)
    cv = c_gate * v
    conv = lax.conv_general_dilated(
        cv, w_conv[:, None, :].astype(cv.dtype), window_strides=(1,),
        padding=[(CONV_WIDTH - 1, 0)],
        dimension_numbers=('NWC', 'WIO', 'NWC'), feature_group_count=dm)
    return (b_gate * conv) @ w_out


def swiglu_ffn(u, w_in, w_out):
    gate, up = jnp.split(u @ w_in, 2, axis=-1)
    return (jax.nn.silu(gate) * up) @ w_out


def setup_inputs(seed: int = 0) -> dict:
    key = jax.random.key(seed)
    ks = iter(jax.random.split(key, 32))
    f32 = jnp.float32

    def nrm(shape, scale):
        return jax.random.normal(next(ks), shape, f32) * scale

    na, nb = N_SSM_LAYERS, N_CONV_LAYERS
    g, pdim, k = SSM_GROUPS, SSM_STATE, SSM_GROUP
    x = nrm((BATCH, SEQ, D_MODEL), 1.0)
    p = nrm((DEPTH, BATCH, SEQ, PLE_DIM), 1.0)
    norm_mix_g = 1.0 + nrm((DEPTH, D_MODEL), 0.02)
    s5_a_re = -0.5 + nrm((na, g, pdim), 0.01)
    s5_a_im = math.pi * jnp.arange(pdim, dtype=f32) + nrm((na, g, pdim), 0.01)
    s5_log_dt = jax.random.uniform(next(ks), (na, g), f32, math.log(DT_MIN), math.log(DT_MAX))
    s5_b_re = nrm((na, g, pdim, k), (2 * k) ** -0.5)
    s5_b_im = nrm((na, g, pdim, k), (2 * k) ** -0.5)
    s5_c_re = nrm((na, g, k, pdim), pdim ** -0.5)
    s5_c_im = nrm((na, g, k, pdim), pdim ** -0.5)
    s5_d = nrm((na, D_MODEL), 1.0)
    s5_w_glu = nrm((na, D_MODEL, 2 * D_MODEL), D_MODEL ** -0.5)
    conv_w_in = nrm((nb, D_MODEL, 3 * D_MODEL), D_MODEL ** -0.5)
    conv_w = nrm((nb, CONV_WIDTH, D_MODEL), CONV_WIDTH ** -0.5)
    conv_w_out = nrm((nb, D_MODEL, D_MODEL), D_MODEL ** -0.5)
    norm_ffn_g = 1.0 + nrm((DEPTH, D_MODEL), 0.02)
    ffn_w_in = nrm((DEPTH, D_MODEL, 2 * FFN_HIDDEN), D_MODEL ** -0.5)
    ffn_w_out = nrm((DEPTH, FFN_HIDDEN, D_MODEL), FFN_HIDDEN ** -0.5)
    norm_ple_g = 1.0 + nrm((DEPTH, D_MODEL), 0.02)
    ple_w_gate = nrm((DEPTH, D_MODEL, D_MODEL), D_MODEL ** -0.5)
    ple_w_up = nrm((DEPTH, PLE_DIM, D_MODEL), PLE_DIM ** -0.5)
    final_norm_g = 1.0 + nrm((D_MODEL,), 0.02)
    return {
        "x": x, "p": p, "norm_mix_g": norm_mix_g,
        "s5_a_re": s5_a_re, "s5_a_im": s5_a_im, "s5_log_dt": s5_log_dt,
        "s5_b_re": s5_b_re, "s5_b_im": s5_b_im, "s5_c_re": s5_c_re, "s5_c_im": s5_c_im,
        "s5_d": s5_d, "s5_w_glu": s5_w_glu,
        "conv_w_in": conv_w_in, "conv_w": conv_w, "conv_w_out": conv_w_out,
        "norm_ffn_g": norm_ffn_g, "ffn_w_in": ffn_w_in, "ffn_w_out": ffn_w_out,
        "norm_ple_g": norm_ple_g, "ple_w_gate": ple_w_gate, "ple_w_up": ple_w_up,
        "final_norm_g": final_norm_g,
    }


def reference(x, p, norm_mix_g, s5_a_re, s5_a_im, s5_log_dt, s5_b_re, s5_b_im,
              s5_c_re, s5_c_im, s5_d, s5_w_glu, conv_w_in, conv_w, conv_w_out,
              norm_ffn_g, ffn_w_in, ffn_w_out, norm_ple_g, ple_w_gate, ple_w_up,
              final_norm_g):
    h = x
    for i in range(DEPTH):
        j = i // N_MIXERS
        hn = rmsnorm(h, norm_mix_g[i])
        if i % N_MIXERS == 0:
            mix = s5_mixer(hn, s5_a_re[j], s5_a_im[j], s5_log_dt[j], s5_b_re[j], s5_b_im[j],
                           s5_c_re[j], s5_c_im[j], s5_d[j], s5_w_glu[j])
        else:
            mix = shortconv_mixer(hn, conv_w_in[j], conv_w[j], conv_w_out[j])
        h = h + mix
        h = h + swiglu_ffn(rmsnorm(h, norm_ffn_g[i]), ffn_w_in[i], ffn_w_out[i])
        gate = jax.nn.sigmoid(rmsnorm(h, norm_ple_g[i]) @ ple_w_gate[i])
        h = h + gate * (p[i] @ ple_w_up[i])
    return rmsnorm(h, final_norm_g)
```

```python
import contextlib
import math
import numpy as np
import concourse.bass as bass
import concourse.mybir as mybir
from concourse.bass_utils import run_bass_kernel_spmd

F32 = mybir.dt.float32
BF16 = mybir.dt.bfloat16
I32 = mybir.dt.int32
AF = mybir.ActivationFunctionType
ALU = mybir.AluOpType

D = 1024
FC = 8
SEQ = 4096
TT = 512
CN = TT // 8
HID = 2816
HC = 22
PLE = 256
DEPTH = 4
EPS = 1e-6
NSLOT = 5
import os
S5STOP = int(os.environ.get('S5STOP', '99'))
DBG = int(os.environ.get('KDBG', '0'))
REC_ENG = os.environ.get('REC_ENG', 'dve')
TWO_PI = float(2 * np.pi)


class Buf:
    __slots__ = ("name", "w", "r", "dsem", "dcnt")

    def __init__(self, name):
        self.name = name
        self.w = None
        self.r = {}
        self.dsem = None
        self.dcnt = 0


class Eng:
    def __init__(self, fw, name, e):
        self.name = name
        self.e = e
        self.sem = fw.new_sem("s_" + name)
        self.cnt = 0
        self.seen = {}


class FW:
    SEM_ROLL = 30000

    def __init__(self, nc, stack):
        self.nc = nc
        self.stack = stack
        self.nsem = 0
        self.pe = Eng(self, "pe", nc.tensor)
        self.act = Eng(self, "act", nc.scalar)
        self.dve = Eng(self, "dve", nc.vector)
        self.pool = Eng(self, "pool", nc.gpsimd)
        self.sp = Eng(self, "sp", nc.sync)

    def new_sem(self, name):
        self.nsem += 1
        return self.stack.enter_context(self.nc.semaphore("%s_%d" % (name, self.nsem)))

    def sbuf(self, name, shape, dtype):
        return self.stack.enter_context(self.nc.sbuf_tensor(name, shape, dtype))

    def psum(self, name, shape, dtype):
        return self.stack.enter_context(self.nc.psum_tensor(name, shape, dtype))

    @staticmethod
    def _flat(bs):
        out = []
        for b in bs:
            if isinstance(b, (list, tuple)):
                out.extend(FW._flat(b))
            else:
                out.append(b)
        return out

    def _deps(self, reads, writes):
        deps = {}

        def add(ev):
            if ev is None:
                return
            k = id(ev[0])
            if k not in deps or deps[k][1] < ev[1]:
                deps[k] = ev
        for b in reads:
            add(b.w)
        for b in writes:
            add(b.w)
            for ev in b.r.values():
                add(ev)
        return deps

    def _wait(self, eng, deps):
        for k, (s, v) in deps.items():
            if eng.seen.get(k, 0) < v:
                eng.e.wait_ge(s, v)
                eng.seen[k] = v

    def _record(self, ev, reads, writes):
        k = id(ev[0])
        for b in reads:
            if k not in b.r or b.r[k][1] < ev[1]:
                b.r[k] = ev
        for b in writes:
            b.w = ev
            b.r = {}

    def op(self, eng, emit, reads=(), writes=()):
        reads = self._flat(reads)
        writes = self._flat(writes)
        self._wait(eng, self._deps(reads, writes))
        ins = emit(eng.e)
        if eng.cnt >= self.SEM_ROLL:
            eng.sem = self.new_sem("s_" + eng.name)
            eng.cnt = 0
        eng.cnt += 1
        ins.then_inc(eng.sem, 1)
        ev = (eng.sem, eng.cnt)
        self._record(ev, reads, writes)
        return ev

    def dma(self, eng, out, in_, reads=(), writes=(), owner=None, **kw):
        reads = self._flat(reads)
        writes = self._flat(writes)
        self._wait(eng, self._deps(reads, writes))
        if owner is None:
            owner = writes[0] if writes else reads[0]
        if owner.dsem is None:
            owner.dsem = self.new_sem("d_" + owner.name)
        ins = eng.e.dma_start(out=out, in_=in_, **kw)
        owner.dcnt += 16
        ins.then_inc(owner.dsem, 16)
        ev = (owner.dsem, owner.dcnt)
        self._record(ev, reads, writes)
        return ev

    def wait_all(self, eng, bufs):
        deps = self._deps(bufs, bufs)
        self._wait(eng, deps)


class Prog:
    def __init__(self, nc, st, ntiles):
        self.nc = nc
        self.fw = FW(nc, st)
        self.ntiles = ntiles
        fw = self.fw
        dt = nc.dram_tensor
        self.x = dt("x", [SEQ, D], F32, kind="ExternalInput").ap()
        self.p = dt("p", [DEPTH, SEQ, PLE], F32, kind="ExternalInput").ap()
        self.vecs = dt("vecs", [168, 128], F32, kind="ExternalInput").ap()
        self.cst = dt("cst", [128, 8], F32, kind="ExternalInput").ap()
        self.s5_a_re = dt("s5_a_re", [2, 64, 64], F32, kind="ExternalInput").ap()
        self.s5_a_im = dt("s5_a_im", [2, 64, 64], F32, kind="ExternalInput").ap()
        self.s5_log_dt = dt("s5_log_dt", [2, 64], F32, kind="ExternalInput").ap()
        self.s5_b_re = dt("s5_b_re", [2, 64, 64, 16], F32, kind="ExternalInput").ap()
        self.s5_b_im = dt("s5_b_im", [2, 64, 64, 16], F32, kind="ExternalInput").ap()
        self.s5_c_re = dt("s5_c_re", [2, 64, 16, 64], F32, kind="ExternalInput").ap()
        self.s5_c_im = dt("s5_c_im", [2, 64, 16, 64], F32, kind="ExternalInput").ap()
        self.w_glu = dt("s5_w_glu", [2, D, 2 * D], F32, kind="ExternalInput").ap()
        self.cw_in = dt("conv_w_in", [2, D, 3 * D], F32, kind="ExternalInput").ap()
        self.cw_out = dt("conv_w_out", [2, D, D], F32, kind="ExternalInput").ap()
        self.f_in = dt("ffn_w_in", [DEPTH, D, 2 * HID], F32, kind="ExternalInput").ap()
        self.f_out = dt("ffn_w_out", [DEPTH, HID, D], F32, kind="ExternalInput").ap()
        self.pg = dt("ple_w_gate", [DEPTH, D, D], F32, kind="ExternalInput").ap()
        self.pu = dt("ple_w_up", [DEPTH, PLE, D], F32, kind="ExternalInput").ap()
        self.out = dt("out", [SEQ, D], F32, kind="ExternalOutput").ap()
        self.dbg = dt("dbg", [128, 16384], F32, kind="ExternalOutput").ap() if DBG else None
        self.B_dbg = Buf("dbg")
        self.glu_b = [dt("glu_b%d" % j, [D, 2 * D], BF16, kind="Internal").ap() for j in range(2)]
        self.cwi_b = [dt("cwi_b%d" % j, [D, 3 * D], BF16, kind="Internal").ap() for j in range(2)]
        self.cwo_b = [dt("cwo_b%d" % j, [D, D], BF16, kind="Internal").ap() for j in range(2)]
        self.fin_b = [dt("fin_b%d" % l, [D, 2 * HID], BF16, kind="Internal").ap() for l in range(DEPTH)]
        self.fout_b = [dt("fout_b%d" % l, [HID, D], BF16, kind="Internal").ap() for l in range(DEPTH)]
        self.pg_b = [dt("pg_b%d" % l, [D, D], BF16, kind="Internal").ap() for l in range(DEPTH)]
        self.pu_b = [dt("pu_b%d" % l, [PLE, D], BF16, kind="Internal").ap() for l in range(DEPTH)]
        self.w1s = [dt("w1s%d" % j, [128, 16384], BF16, kind="Internal").ap() for j in range(2)]
        self.w2s = [dt("w2s%d" % j, [128, 16384], BF16, kind="Internal").ap() for j in range(2)]
        self.w4s = [dt("w4s%d" % j, [128, 8192], BF16, kind="Internal").ap() for j in range(2)]
        B = Buf
        self.B_glu = [B("glu%d" % j) for j in range(2)]
        self.B_cwi = [B("cwi%d" % j) for j in range(2)]
        self.B_cwo = [B("cwo%d" % j) for j in range(2)]
        self.B_fin = [B("fin%d" % l) for l in range(DEPTH)]
        self.B_fout = [B("fout%d" % l) for l in range(DEPTH)]
        self.B_pg = [B("pg%d" % l) for l in range(DEPTH)]
        self.B_pu = [B("pu%d" % l) for l in range(DEPTH)]
        self.B_w1s = [B("w1s%d" % j) for j in range(2)]
        self.B_w2s = [B("w2s%d" % j) for j in range(2)]
        self.B_w4s = [B("w4s%d" % j) for j in range(2)]
        self.B_out = B("out")
        sb = fw.sbuf
        self.ring = [sb("ring%d" % i, [128, 8192], BF16) for i in range(NSLOT)]
        self.B_ring = [B("ring%d" % i) for i in range(NSLOT)]
        self.ring_i = 0
        self.T_h = sb("T_h", [128, FC, TT], F32); self.B_h = [B("h%d" % i) for i in range(FC)]
        self.T_xn = sb("T_xn", [128, FC, TT], BF16); self.B_xn = [B("xn%d" % i) for i in range(FC)]
        self.T_sq = sb("T_sq", [128, FC, TT], BF16); self.B_sq = [B("sq%d" % i) for i in range(FC)]
        self.pending = []
        self.stat_n = 0
        self.T_a32 = sb("T_a32", [128, 4096], F32); self.B_a32 = B("a32")
        self.T_big = sb("T_big", [128, 11264], BF16); self.B_big = B("big")
        self.T_c32 = sb("T_c32", [128, 6272], F32); self.B_c32 = B("c32")
        self.T_d32 = sb("T_d32", [128, 4096], F32)
        self.T_e16 = sb("T_e16", [128, 4096], BF16); self.B_e16 = B("e16")
        self.T_rstd = sb("T_rstd", [128, TT], F32); self.B_rstd = B("rstd")
        self.vecT = sb("vecT", [128, 168], F32); self.B_vecT = B("vecT")
        self.cstT = sb("cstT", [128, 8], F32); self.B_cst = B("cst")
        self.ident = sb("ident", [128, 128], F32); self.B_id = B("ident")
        self.identb = sb("identb", [128, 128], BF16); self.B_idb = B("identb")
        self.ones = sb("ones", [128, 128], BF16); self.B_ones = B("ones")
        self.ccar = [sb("ccar%d" % j, [128, FC, 2], F32) for j in range(2)]
        self.B_ccar = [B("ccar%d" % j) for j in range(2)]
        self.scar = [sb("scar%d" % j, [128, 96], F32) for j in range(2)]
        self.B_scar = [B("scar%d" % j) for j in range(2)]
        self.A1 = [sb("A1_%d" % j, [128, 64], F32) for j in range(2)]
        self.A2 = [sb("A2_%d" % j, [128, 64], F32) for j in range(2)]
        self.B_A = [B("A%d" % j) for j in range(2)]
        d = self.T_d32
        self.sg = [d[:, 0:512], d[:, 512:1024]]
        self.B_sg = [B("sg0"), B("sg1")]
        self.sgi = 0
        self.tmp = [d[:, 1024:1536], d[:, 1536:2048]]
        self.B_tmp = [B("tmp0"), B("tmp1")]
        self.tmpi = 0
        self.cvm = [d[:, 2048:2562], d[:, 2562:3076]]
        self.B_cvm = [B("cvm0"), B("cvm1")]
        self.pt = d[:, 3076:4096]
        self.B_d32_all = self.B_sg + self.B_tmp + self.B_cvm
        self.pT = sb("pT", [128, 2, TT], BF16); self.B_pT = B("pT")
        self.fz = sb("fz", [128, 8], F32)
        self.B_mark = [None, None]
        self.B_htail = Buf("htail")
        self.banks = [fw.psum("bank%d" % i, [128, 512], F32) for i in range(8)]
        self.B_banks = [B("bank%d" % i) for i in range(8)]
        self.bank_i = 0

    def bank(self):
        i = self.bank_i
        self.bank_i = (i + 1) % 7
        return self.banks[i], self.B_banks[i]

    def h_written(self, m):
        self.actf(self.T_sq[:, m, :], self.T_h[:, m, :], AF.Square, [self.B_h[m]], [self.B_sq[m]])
        self.pending.append(m)
        self.flush_stats(2)

    def flush_stats(self, keep):
        while len(self.pending) > keep:
            m = self.pending.pop(0)
            n = self.stat_n
            self.mm([(self.banks[7][:], self.ones[:], self.T_sq[:, m, :], n == 0, n == FC - 1, None)],
                    [self.B_ones, self.B_sq[m]], [self.B_banks[7]])
            self.stat_n = (n + 1) % FC

    def slot(self):
        i = self.ring_i
        self.ring_i = (i + 1) % NSLOT
        return self.ring[i], self.B_ring[i]

    def next_sg(self):
        i = self.sgi
        self.sgi ^= 1
        return self.sg[i], self.B_sg[i]

    def next_tmp(self):
        i = self.tmpi
        self.tmpi ^= 1
        return self.tmp[i], self.B_tmp[i]

    def mm(self, mms, reads, writes):
        def emit(e):
            ins = None
            for (o, l, r, st, sp, tp) in mms:
                if tp is None:
                    ins = e.matmul(o, l, r, start=st, stop=sp)
                else:
                    ins = e.matmul(o, l, r, start=st, stop=sp, tile_position=tp)
            return ins
        return self.fw.op(self.fw.pe, emit, reads, writes)

    def tr(self, trs, reads, writes):
        def emit(e):
            ins = None
            for (o, i, idn) in trs:
                ins = e.transpose(o, i, idn)
            return ins
        return self.fw.op(self.fw.pe, emit, reads, writes)

    def E(self, name):
        return getattr(self.fw, name)

    def tt(self, eng, out, in0, in1, op, reads, writes):
        return self.fw.op(self.E(eng), lambda e: e.tensor_tensor(out=out, in0=in0, in1=in1, op=op), reads, writes)

    def ts(self, eng, out, in0, s1, op0, reads, writes, s2=None, op1=None):
        if op1 is None:
            return self.fw.op(self.E(eng), lambda e: e.tensor_scalar(out=out, in0=in0, scalar1=s1, scalar2=None, op0=op0), reads, writes)
        return self.fw.op(self.E(eng), lambda e: e.tensor_scalar(out=out, in0=in0, scalar1=s1, scalar2=s2, op0=op0, op1=op1), reads, writes)

    def stt(self, out, in0, scalar, in1, op0, op1, reads, writes):
        return self.fw.op(self.fw.dve, lambda e: e.scalar_tensor_tensor(out=out, in0=in0, scalar=scalar, in1=in1, op0=op0, op1=op1), reads, writes)

    def actf(self, out, in_, func, reads, writes, scale=1.0, bias=0.0):
        return self.fw.op(self.fw.act, lambda e: e.activation(out=out, in_=in_, func=func, scale=scale, bias=bias), reads, writes)

    def cp(self, eng, out, in_, reads, writes):
        if eng == "act":
            return self.fw.op(self.fw.act, lambda e: e.copy(out=out, in_=in_), reads, writes)
        return self.fw.op(self.E(eng), lambda e: e.tensor_copy(out=out, in_=in_), reads, writes)

    def memset(self, eng, ap, val, writes):
        return self.fw.op(self.E(eng), lambda e: e.memset(ap, val), (), writes)

    def fence(self):
        bufs = ([self.B_a32, self.B_big, self.B_c32, self.B_e16, self.B_rstd, self.B_xn, self.B_h]
                + self.B_d32_all + self.B_ring + self.B_banks + [self.B_htail])
        self.fw.op(self.fw.dve, lambda e: e.memset(self.fz[:], 0.0), bufs, bufs)

    def cast_weights(self, layers, gate=None):
        fw = self.fw
        first = [True]

        def cast(dst, src, B, rows):
            for r0 in range(0, rows, 128):
                rd = [gate] if (gate is not None and first[0]) else []
                first[0] = False
                fw.dma(fw.pool, dst[r0:r0 + 128, :], src[r0:r0 + 128, :], reads=rd, writes=[B], owner=B)
        order = []
        for l in layers:
            j = l // 2
            if l % 2 == 0:
                order.append((self.glu_b[j], self.w_glu[j], self.B_glu[j], D))
            else:
                order.append((self.cwi_b[j], self.cw_in[j], self.B_cwi[j], D))
                order.append((self.cwo_b[j], self.cw_out[j], self.B_cwo[j], D))
            order.append((self.fin_b[l], self.f_in[l], self.B_fin[l], D))
            order.append((self.fout_b[l], self.f_out[l], self.B_fout[l], HID))
            order.append((self.pg_b[l], self.pg[l], self.B_pg[l], D))
            order.append((self.pu_b[l], self.pu[l], self.B_pu[l], PLE))
        for a in order:
            cast(*a)

    def consts(self):
        fw = self.fw
        self.memset("pool", self.ident[:], 0.0, [self.B_id])
        fw.op(fw.pool, lambda e: e.affine_select(out=self.ident[:], in_=self.ident[:], pattern=[[-1, 128]],
                                                  compare_op=ALU.not_equal, fill=1.0, base=0, channel_multiplier=1),
              [self.B_id], [self.B_id])
        self.cp("dve", self.identb[:], self.ident[:], [self.B_id], [self.B_idb])
        self.memset("dve", self.ones[:], 1.0, [self.B_ones])
        fw.dma(fw.sp, self.cstT[:], self.cst, writes=[self.B_cst])
        for j in range(2):
            self.memset("dve", self.ccar[j][:], 0.0, [self.B_ccar[j]])
            self.memset("dve", self.scar[j][:], 0.0, [self.B_scar[j]])
        v = self.T_a32
        fw.dma(fw.sp, v[:, 0:128], self.vecs[0:128, :], writes=[self.B_a32])
        fw.dma(fw.sp, v[0:40, 128:256], self.vecs[128:168, :], writes=[self.B_a32])
        bk, Bb = self.bank()
        self.tr([(bk[:, 0:128], v[:, 0:128], self.ident[:]),
                 (bk[:, 128:168], v[0:40, 128:256], self.ident[0:40, 0:40])],
                [self.B_a32, self.B_id], [Bb])
        self.cp("dve", self.vecT[:], bk[:, 0:168], [Bb], [self.B_vecT])

    def s5_prologue(self, j):
        fw = self.fw
        nc = self.nc
        c32 = self.T_c32
        Bc = self.B_c32
        R = lambda i: c32[:, i * 512:(i + 1) * 512]
        d32 = self.T_d32
        Bd = self.B_sg[0]
        Pt = lambda i: d32[:, 2048 + i * 64: 2048 + (i + 1) * 64]
        cT_re = d32[:, 0:1024]
        cT_im = d32[:, 1024:2048]
        a32 = self.T_a32
        Ba = self.B_a32
        big32 = self.T_big[:].bitcast(F32)
        Bg = self.B_big

        def G(tau):
            if tau < 4:
                return a32[:, tau * 1024:(tau + 1) * 1024]
            return big32[:, (tau - 4) * 1024:(tau - 3) * 1024]

        def GB(tau):
            return Ba if tau < 4 else Bg
        slA, BA = self.slot()
        slB, BB = self.slot()
        slC, BC = self.slot()
        slD, BD = self.slot()
        slE, BE = self.slot()
        C32 = slC[:].bitcast(F32)
        D32 = slD[:].bitcast(F32)
        b_re_P = C32[:, 0:1024]
        b_im_P = C32[:, 1024:2048]
        Bst = C32[:, 2048:3072]
        Cin = C32[:, 3072:4096]
        Bst_pad = D32[:, 0:2048]
        KoutS = D32[:, 2048:4096]

        Ain = R(0)
        for (src, c0) in ((self.s5_a_re[j], 0), (self.s5_a_im[j], 128)):
            fw.dma(fw.sp, Ain[0:64, c0:c0 + 64], src, writes=[Bc])
            fw.dma(fw.sp, Ain[0:64, c0 + 64:c0 + 128], src, writes=[Bc])
        bk, Bb = self.bank()
        self.tr([(bk[:, 0:64], Ain[0:64, 0:128], self.ident[0:64, 0:64]),
                 (bk[:, 64:128], Ain[0:64, 128:256], self.ident[0:64, 0:64])], [Bc, self.B_id], [Bb])
        are_P, aim_P, ldt_P = Pt(0), Pt(1), Pt(2)
        self.cp("dve", d32[:, 2048:2176], bk[:, 0:128], [Bb], [Bd])
        fw.dma(fw.sp, ldt_P, self.s5_log_dt[j].partition_broadcast(128), writes=[Bd])
        are_R, aim_R = R(1), R(2)
        ldt_R8 = Pt(3)[:, 0:8]
        for gl in range(8):
            for (src, dst) in ((self.s5_a_re[j], are_R), (self.s5_a_im[j], aim_R)):
                fw.dma(fw.sp, dst[16 * gl:16 * gl + 16, :].rearrange("k (fc p) -> k fc p", fc=8),
                       src.rearrange("(fc gl) p -> gl fc p", gl=8)[gl].partition_broadcast(16), writes=[Bc])
            fw.dma(fw.sp, ldt_R8[16 * gl:16 * gl + 16, :],
                   self.s5_log_dt[j].rearrange("(fc gl) -> gl fc", gl=8)[gl].partition_broadcast(16), writes=[Bd],
                   allow_slow_non_contiguous=True)
        for (src, dst) in ((self.s5_b_re[j], b_re_P), (self.s5_b_im[j], b_im_P)):
            sv = src.rearrange("g p k -> p g k")
            dv = dst.rearrange("p (g k) -> p g k", k=16)
            for half in range(2):
                for g0 in range(0, 64, 16):
                    fw.dma(fw.sp, dv[64 * half:64 * half + 64, g0:g0 + 16, :], sv[:, g0:g0 + 16, :], writes=[BC])
        b_re_R, b_im_R = R(3), R(4)
        for (srcP, dstR) in ((b_re_P, b_re_R), (b_im_P, b_im_R)):
            bk, Bb = self.bank()
            self.tr([(bk[:, fc * 64:(fc + 1) * 64], srcP[0:64, fc * 128:(fc + 1) * 128], self.ident[0:64, 0:64])
                     for fc in range(8)], [BC, self.B_id], [Bb])
            self.cp("dve", dstR, bk[:], [Bb], [Bc])
        for (src, dstT) in ((self.s5_c_re[j], cT_re), (self.s5_c_im[j], cT_im)):
            sv = src.rearrange("g j p -> (g j) p").rearrange("(r q) p -> q r p", q=128)
            cv = Cin.rearrange("q (r c) -> q r c", c=128)
            fw.dma(fw.sp, cv[:, :, 0:64], sv, writes=[BC])
            fw.dma(fw.sp, cv[:, :, 64:128], sv, writes=[BC])
            for hb in range(2):
                bk, Bb = self.bank()
                self.tr([(bk[:, r * 128:(r + 1) * 128], cv[:, hb * 4 + r, :], self.ident[:]) for r in range(4)],
                        [BC, self.B_id], [Bb])
                self.cp("dve", dstT[:, hb * 512:(hb + 1) * 512], bk[:], [Bb], [Bd])

        self.B_mark[j] = Buf("mark%d" % j)
        self.fw.op(self.fw.dve, lambda e: e.memset(self.fz[:, 0:4], 0.0), [Bc, Bd, BC], [self.B_mark[j]])
        def ctab(N, are, aim, dtb, T, B):
            E = "dve"
            x, y, k_i, m1, den = T(4), T(5), T(6), T(7), T(8)
            ab_re, ab_im, f_re, f_im = T(0), T(1), T(2), T(3)
            self.tt(E, x, are, dtb, ALU.mult, [B], [B])
            self.actf(x, x, AF.Exp, [B], [B])
            self.tt(E, y, aim, dtb, ALU.mult, [B], [B])
            self.ts(E, y, y, 1.0 / TWO_PI, ALU.mult, [B], [B])

            def sin_turns(dst, src, shift):
                ki = k_i.bitcast(I32)
                self.ts(E, dst, src, float(shift), ALU.add, [B], [B])
                self.cp(E, ki, dst, [B], [B])
                self.cp(E, m1, ki, [B], [B])
                self.tt(E, dst, dst, m1, ALU.subtract, [B], [B])
                self.ts(E, m1, dst, 0.5, ALU.is_gt, [B], [B])
                self.tt(E, dst, dst, m1, ALU.subtract, [B], [B])
                self.ts(E, m1, dst, -0.5, ALU.is_lt, [B], [B])
                self.tt(E, dst, dst, m1, ALU.add, [B], [B])
                self.actf(dst, dst, AF.Sin, [B], [B], scale=TWO_PI)
            sin_turns(ab_im, y, 0.0)
            sin_turns(ab_re, y, 0.25)
            self.tt(E, ab_re, ab_re, x, ALU.mult, [B], [B])
            self.tt(E, ab_im, ab_im, x, ALU.mult, [B], [B])
            self.tt(E, den, are, are, ALU.mult, [B], [B])
            self.tt(E, m1, aim, aim, ALU.mult, [B], [B])
            self.tt(E, den, den, m1, ALU.add, [B], [B])
            self.fw.op(self.fw.dve, lambda e: e.reciprocal(out=den, in_=den), [B], [B])
            self.ts(E, x, ab_re, -1.0, ALU.add, [B], [B])
            self.tt(E, f_re, x, are, ALU.mult, [B], [B])
            self.tt(E, m1, ab_im, aim, ALU.mult, [B], [B])
            self.tt(E, f_re, f_re, m1, ALU.add, [B], [B])
            self.tt(E, f_re, f_re, den, ALU.mult, [B], [B])
            self.tt(E, f_im, ab_im, are, ALU.mult, [B], [B])
            self.tt(E, m1, x, aim, ALU.mult, [B], [B])
            self.tt(E, f_im, f_im, m1, ALU.subtract, [B], [B])
            self.tt(E, f_im, f_im, den, ALU.mult, [B], [B])
            return ab_re, ab_im, f_re, f_im

        def cmul(o_re, o_im, a_re, a_im, b_re, b_im, t1, t2, B):
            E = "dve"
            self.tt(E, t1, a_re, b_re, ALU.mult, [B], [B])
            self.tt(E, t2, a_im, b_im, ALU.mult, [B], [B])
            self.tt(E, t1, t1, t2, ALU.subtract, [B], [B])
            self.tt(E, t2, a_re, b_im, ALU.mult, [B], [B])
            self.tt(E, o_im, a_im, b_re, ALU.mult, [B], [B])
            self.tt(E, o_im, o_im, t2, ALU.add, [B], [B])
            self.cp(E, o_re, t1, [B], [B])

        dtP = Pt(4)
        self.actf(dtP, ldt_P, AF.Exp, [Bd], [Bd])
        PT = lambda i: Pt(5 + i)
        ab_re, ab_im, f_re, f_im = ctab(64, are_P, aim_P, dtP, PT, Bd)
        U, V = Pt(16), Pt(17)
        self.cp("dve", U[0:64, :], f_re[0:64, :], [Bd], [Bd])
        self.cp("dve", U[64:128, :], f_im[64:128, :], [Bd], [Bd])
        self.ts("dve", V[0:64, :], f_im[0:64, :], -1.0, ALU.mult, [Bd], [Bd])
        self.cp("dve", V[64:128, :], f_re[64:128, :], [Bd], [Bd])
        bc16 = lambda t: t.unsqueeze(2).to_broadcast([128, 64, 16])
        v3 = lambda t: t.rearrange("p (g k) -> p g k", k=16)
        tmpK = KoutS
        self.tt("dve", v3(Bst), v3(b_re_P), bc16(U), ALU.mult, [BC, Bd], [BC])
        self.tt("dve", v3(tmpK[:, 0:1024]), v3(b_im_P), bc16(V), ALU.mult, [BC, Bd], [BD])
        self.tt("dve", Bst, Bst, tmpK[:, 0:1024], ALU.add, [BC, BD], [BC])
        self.memset("dve", Bst_pad, 0.0, [BD])
        bp = Bst_pad.rearrange("p (g e k) -> p g e k", e=2, k=16)
        bs = v3(Bst).rearrange("p (gg par) k -> p gg par k", par=2)
        bpp = bp.rearrange("p (gg par) e k -> p gg par e k", par=2)
        self.cp("dve", bpp[:, :, 0, 0, :], bs[:, :, 0, :], [BC], [BD])
        self.cp("dve", bpp[:, :, 1, 1, :], bs[:, :, 1, :], [BC], [BD])
        Pre, Pim, t1, t2, X, Y = Pt(18), Pt(19), Pt(20), Pt(21), Pt(22), Pt(23)
        self.memset("dve", Pre, 1.0, [Bd])
        self.memset("dve", Pim, 0.0, [Bd])
        g3 = lambda t: t.rearrange("p (g k) -> p g k", k=16)
        for tau in range(9):
            if tau > 0:
                cmul(Pre, Pim, Pre, Pim, ab_re, ab_im, t1, t2, Bd)
            self.cp("dve", X[0:64, :], Pre[0:64, :], [Bd], [Bd])
            self.ts("dve", X[64:128, :], Pim[64:128, :], -1.0, ALU.mult, [Bd], [Bd])
            self.cp("dve", Y[0:64, :], Pim[0:64, :], [Bd], [Bd])
            self.cp("dve", Y[64:128, :], Pre[64:128, :], [Bd], [Bd])
            Gt = G(tau)
            BG = GB(tau)
            self.tt("dve", g3(Gt), g3(cT_re), bc16(X), ALU.mult, [Bd], [BG])
            self.tt("dve", g3(tmpK[:, 0:1024]), g3(cT_im), bc16(Y), ALU.mult, [Bd], [BD])
            self.tt("dve", Gt, Gt, tmpK[:, 0:1024], ALU.subtract, [BG, BD], [BG])
        A1, A2 = self.A1[j], self.A2[j]
        BAj = self.B_A[j]
        for half in range(2):
            ps = slice(64 * half, 64 * half + 64)
            src_re = Pre.rearrange("p (gg par) -> p gg par", par=2)[ps, :, half]
            src_im = Pim.rearrange("p (gg par) -> p gg par", par=2)[ps, :, half]
            self.cp("dve", A1[ps, 0:32], src_re, [Bd], [BAj])
            self.cp("dve", A1[ps, 32:64], src_re, [Bd], [BAj])
            self.ts("dve", A2[ps, 0:32], src_im, -1.0, ALU.mult, [Bd], [BAj])
            self.cp("dve", A2[ps, 32:64], src_im, [Bd], [BAj])
        w4v = slE[:].rearrange("p (g t k) -> p g t k", t=8, k=16)
        for t in range(8):
            self.cp("act", w4v[:, :, t, :], g3(G(t + 1)), [GB(t + 1)], [BE])
        fw.dma(fw.sp, self.w4s[j], slE[:], reads=[BE], writes=[self.B_w4s[j]], owner=self.B_w4s[j])
        kb = []
        for i in range(4):
            kb.append(self.bank())
        for g in range(64):
            fc, gl = divmod(g, 8)
            q, e = divmod(gl, 2)
            col = (fc * 2 + e) * 128
            bkk, Bk = kb[col // 512]
            c0 = col % 512
            mms = []
            for half in range(2):
                base = a32 if half == 0 else big32
                rhs = base[:, 0:4096].rearrange("p (tau g k) -> p tau g k", tau=4, k=16)[:, :, g, :]
                mms.append((bkk[32 * q:32 * q + 32, c0 + half * 64:c0 + half * 64 + 64],
                            Bst_pad[:, g * 32:(g + 1) * 32], rhs, True, True, (0, 32 * q)))
            self.mm(mms, [BD, Ba, Bg], [Bk])
        for i in range(4):
            self.cp("act", KoutS[:, i * 512:(i + 1) * 512], kb[i][0][:], [kb[i][1]], [BD])
        dtR8 = Pt(24)[:, 0:8]
        self.actf(dtR8, ldt_R8, AF.Exp, [Bd], [Bd])
        dtR = R(5)
        self.cp("dve", dtR.rearrange("p (fc q) -> p fc q", fc=8), dtR8.unsqueeze(2).to_broadcast([128, 8, 64]), [Bd], [Bc])
        sc = [R(6), R(7), R(8), R(9), R(10), R(11), R(0)]
        extra = self.T_e16[:].bitcast(F32)
        Be = self.B_e16
        sc += [extra[:, i * 512:(i + 1) * 512] for i in range(4)]
        Bboth = None

        class _T:
            pass
        RT = lambda i: sc[i]
        def ctabR():
            saved_tt, saved_ts, saved_cp, saved_actf = self.tt, self.ts, self.cp, self.actf
            return ctab(512, are_R, aim_R, dtR, RT, Bc)
        self.memset("dve", extra[:, 0:2048], 0.0, [Be, Bc])
        abr, abi, fr, fi = ctabR()
        Qre, Qim, q1, q2 = sc[9], sc[10], sc[4], sc[5]
        self.cp("dve", Qre, fr, [Bc], [Bc])
        self.cp("dve", Qim, fi, [Bc], [Bc])
        wre, wim = sc[6], sc[7]
        w2A = slA[:].rearrange("p (fc e s c) -> p fc e s c", fc=4, e=2, s=8)
        w2B = slB[:].rearrange("p (fc e s c) -> p fc e s c", fc=4, e=2, s=8)
        for m in range(8):
            s = 7 - m
            if m > 0:
                cmul(Qre, Qim, Qre, Qim, abr, abi, q1, q2, Bc)
            self.tt("dve", wre, Qre, b_re_R, ALU.mult, [Bc], [Bc])
            self.tt("dve", q1, Qim, b_im_R, ALU.mult, [Bc], [Bc])
            self.tt("dve", wre, wre, q1, ALU.subtract, [Bc], [Bc])
            self.tt("dve", wim, Qim, b_re_R, ALU.mult, [Bc], [Bc])
            self.tt("dve", q1, Qre, b_im_R, ALU.mult, [Bc], [Bc])
            self.tt("dve", wim, wim, q1, ALU.add, [Bc], [Bc])
            for e in range(2):
                msk = self.cstT[:, e:e + 1]
                for (w2, Bw, f0) in ((w2A, BA, 0), (w2B, BB, 4)):
                    for (src, c0) in ((wre, 0), (wim, 64)):
                        self.ts("dve", w2[:, :, e, s, c0:c0 + 64],
                                src.rearrange("p (fc q) -> p fc q", fc=8)[:, f0:f0 + 4, :], msk, ALU.mult,
                                [Bc, self.B_cst], [Bw])
        fw.dma(fw.sp, self.w2s[j][:, 0:8192], slA[:], reads=[BA], writes=[self.B_w2s[j]], owner=self.B_w2s[j])
        fw.dma(fw.sp, self.w2s[j][:, 8192:16384], slB[:], reads=[BB], writes=[self.B_w2s[j]], owner=self.B_w2s[j])
        self.memset("dve", slA[:], 0.0, [BA])
        self.memset("dve", slB[:], 0.0, [BB])
        kv = KoutS.rearrange("p (fe tau k) -> p fe tau k", tau=8, k=16)
        for (sl, Bw, f0) in ((slA, BA, 0), (slB, BB, 8)):
            wv = sl[:].rearrange("p (fe s t k) -> p fe s t k", s=8, t=8, k=16)
            for s in range(8):
                self.cp("act" if s % 2 else "dve", wv[:, :, s, s:8, :], kv[:, f0:f0 + 8, 0:8 - s, :], [BD], [Bw])
        fw.dma(fw.sp, self.w1s[j][:, 0:8192], slA[:], reads=[BA], writes=[self.B_w1s[j]], owner=self.B_w1s[j])
        fw.dma(fw.sp, self.w1s[j][:, 8192:16384], slB[:], reads=[BB], writes=[self.B_w1s[j]], owner=self.B_w1s[j])

    def wslot(self, pieces):
        sl, Bs = self.slot()
        for (dstf, src, Bsrc) in pieces:
            self.fw.dma(self.fw.sp, dstf(sl), src, reads=[Bsrc], writes=[Bs])
        return sl, Bs

    def wblock(self, src, Bsrc, kc, c0, w, off=0):
        return (lambda sl: sl[:, off:off + kc * w].rearrange("p (kc m) -> p kc m", kc=kc),
                src.rearrange("(kc p) m -> p kc m", p=128)[:, :, c0:c0 + w], Bsrc)

    def rmsnorm(self, col, want32=False):
        h = self.T_h
        self.flush_stats(0)
        assert self.stat_n == 0
        bk, Bb = self.banks[7], self.B_banks[7]
        self.actf(self.T_rstd[:], bk[:], AF.Sqrt, [Bb], [self.B_rstd], scale=1.0 / D, bias=EPS)
        self.fw.op(self.fw.dve, lambda e: e.reciprocal(out=self.T_rstd[:], in_=self.T_rstd[:]), [self.B_rstd], [self.B_rstd])
        if want32:
            x32 = self.T_a32[:].rearrange("p (f t) -> p f t", f=FC)
            for fc in range(FC):
                self.stt(x32[:, fc, :], h[:, fc, :], self.vecT[:, col + fc:col + fc + 1], self.T_rstd[:],
                         ALU.mult, ALU.mult, [self.B_h[fc], self.B_rstd, self.B_vecT], [self.B_a32])
                self.cp("act", self.T_xn[:, fc, :], x32[:, fc, :], [self.B_a32], [self.B_xn[fc]])
        else:
            for fc in range(FC):
                self.stt(self.T_xn[:, fc, :], h[:, fc, :], self.vecT[:, col + fc:col + fc + 1], self.T_rstd[:],
                         ALU.mult, ALU.mult, [self.B_h[fc], self.B_rstd, self.B_vecT], [self.B_xn[fc]])

    def proj8(self, sl, Bs, off, w, m, rhs, Brhs, kc=FC, split=False):
        bk, Bb = self.bank()
        mms = [(bk[:], sl[:, off + k * w + m * 128: off + k * w + (m + 1) * 128], rhs[:, k, :], k == 0, k == kc - 1, None)
               for k in range(kc)]
        if split:
            for k in range(kc):
                wr = [Bb] if (k == 0 or k == kc - 1) else []
                self.mm([mms[k]], [Bs, Brhs[k]], wr)
        else:
            self.mm(mms, [Bs, Brhs], [Bb])
        return bk, Bb

    def resid_add(self, m, src, Bsrc):
        self.tt("dve", self.T_h[:, m, :], self.T_h[:, m, :], src, ALU.add, [self.B_h[m], Bsrc], [self.B_h[m]])
        self.h_written(m)

    def ffn(self, l):
        self.rmsnorm(32 + l * 8)
        hid = self.T_big[:].rearrange("p (j t) -> p j t", t=TT)
        for (j0, nj) in ((0, 4), (4, 4), (8, 4), (12, 4), (16, 4), (20, 2)):
            w = nj * 128
            sl, Bs = self.wslot([self.wblock(self.fin_b[l], self.B_fin[l], FC, j0 * 128, w, 0),
                                 self.wblock(self.fin_b[l], self.B_fin[l], FC, HID + j0 * 128, w, 4096)])
            for jj in range(nj):
                bg, Bg = self.proj8(sl, Bs, 0, w, jj, self.T_xn, self.B_xn, split=(j0 + jj == 0))
                bu, Bu = self.proj8(sl, Bs, 4096, w, jj, self.T_xn, self.B_xn)
                sg, Bsg = self.next_sg()
                self.actf(sg, bg[:], AF.Silu, [Bg], [Bsg])
                self.tt("dve", hid[:, j0 + jj, :], sg, bu[:], ALU.mult, [Bsg, Bu], [self.B_big])
        for mp in range(4):
            sl, Bs = self.wslot([(lambda s: s[:, 0:HC * 256].rearrange("p (hc m) -> p hc m", hc=HC),
                                  self.fout_b[l].rearrange("(hc p) m -> p hc m", p=128)[:, :, mp * 256:(mp + 1) * 256],
                                  self.B_fout[l])])
            for mi in range(2):
                bk, Bb = self.bank()
                self.mm([(bk[:], sl[:, hc * 256 + mi * 128: hc * 256 + (mi + 1) * 128], hid[:, hc, :], hc == 0, hc == HC - 1, None)
                         for hc in range(HC)], [Bs, self.B_big], [Bb])
                self.resid_add(mp * 2 + mi, bk[:], Bb)

    def ple(self, l, tile):
        fw = self.fw
        tok0 = tile * TT
        pt = self.T_c32[:, 0:1024].rearrange("p (j f) -> p j f", j=4)
        fw.dma(fw.sp, pt, self.p[l, tok0:tok0 + TT, :].rearrange("(j q) f -> q j f", q=128), writes=[self.B_c32])
        for kc in range(2):
            bk, Bb = self.bank()
            self.tr([(bk[:, jj * 128:(jj + 1) * 128], pt[:, jj, kc * 128:(kc + 1) * 128], self.ident[:]) for jj in range(4)],
                    [self.B_c32, self.B_id], [Bb])
            self.cp("act", self.pT[:, kc, :], bk[:], [Bb], [self.B_pT])
        self.rmsnorm(64 + l * 8)
        slg, Bsg_ = self.wslot([self.wblock(self.pg_b[l], self.B_pg[l], FC, 0, D, 0)])
        slu, Bsu = self.wslot([self.wblock(self.pu_b[l], self.B_pu[l], 2, 0, D, 0)])
        for m in range(FC):
            bg, Bg = self.proj8(slg, Bsg_, 0, D, m, self.T_xn, self.B_xn, split=(m == 0))
            bu, Bu = self.proj8(slu, Bsu, 0, D, m, self.pT, self.B_pT, kc=2)
            sg, Bsg = self.next_sg()
            self.actf(sg, bg[:], AF.Sigmoid, [Bg], [Bsg])
            tm, Btm = self.next_tmp()
            self.tt("dve", tm, sg, bu[:], ALU.mult, [Bsg, Bu], [Btm])
            self.resid_add(m, tm, Btm)

    def conv(self, j, l):
        self.rmsnorm(l * 8)
        slc, Bsc = self.wslot([self.wblock(self.cwi_b[j], self.B_cwi[j], FC, D, D, 0)])
        slv, Bsv = self.wslot([self.wblock(self.cwi_b[j], self.B_cwi[j], FC, 2 * D, D, 0)])
        slb, Bsb = self.wslot([self.wblock(self.cwi_b[j], self.B_cwi[j], FC, 0, D, 0)])
        acc = self.T_c32[:, 0:512]
        bcv = self.T_e16[:].rearrange("p (f t) -> p f t", f=FC)
        wcol = lambda i, m: self.vecT[:, 120 + (j * 3 + i) * 8 + m: 120 + (j * 3 + i) * 8 + m + 1]
        for m in range(FC):
            bc_, Bbc = self.proj8(slc, Bsc, 0, D, m, self.T_xn, self.B_xn, split=(m == 0))
            bv, Bbv = self.proj8(slv, Bsv, 0, D, m, self.T_xn, self.B_xn)
            bb, Bbb = self.proj8(slb, Bsb, 0, D, m, self.T_xn, self.B_xn)
            sg, Bsg = self.next_sg()
            self.cp("act", sg, bc_[:], [Bbc], [Bsg])
            cvm, Bcv = self.cvm[m % 2], self.B_cvm[m % 2]
            self.cp("pool", cvm[:, 0:2], self.ccar[j][:, m, :], [self.B_ccar[j]], [Bcv])
            self.tt("dve", cvm[:, 2:2 + TT], sg, bv[:], ALU.mult, [Bsg, Bbv], [Bcv])
            self.cp("pool", self.ccar[j][:, m, :], cvm[:, TT:TT + 2], [Bcv], [self.B_ccar[j]])
            self.ts("dve", acc, cvm[:, 2:2 + TT], wcol(2, m), ALU.mult, [Bcv, self.B_vecT], [self.B_c32])
            self.stt(acc, cvm[:, 1:1 + TT], wcol(1, m), acc, ALU.mult, ALU.add, [Bcv, self.B_vecT, self.B_c32], [self.B_c32])
            self.stt(acc, cvm[:, 0:TT], wcol(0, m), acc, ALU.mult, ALU.add, [Bcv, self.B_vecT, self.B_c32], [self.B_c32])
            self.tt("dve", bcv[:, m, :], acc, bb[:], ALU.mult, [self.B_c32, Bbb], [self.B_e16])
        slo, Bso = self.wslot([self.wblock(self.cwo_b[j], self.B_cwo[j], FC, 0, D, 0)])
        for m in range(FC):
            bk, Bb = self.proj8(slo, Bso, 0, D, m, bcv, self.B_e16)
            self.resid_add(m, bk[:], Bb)

    def s5(self, j, l):
        fw = self.fw
        self.rmsnorm(l * 8, want32=True)
        u = self.T_xn
        Bu = self.B_xn
        u8 = lambda fc: u[:, fc, :].rearrange("p (c s) -> p s c", s=8)
        w2sl = []
        for hf in range(2):
            w2sl.append(self.wslot([(lambda s: s[:], self.w2s[j][:, hf * 8192:(hf + 1) * 8192], self.B_w2s[j])]))
        SA = self.T_big[:].bitcast(F32)[:, 0:4096].rearrange("p (r g c) -> p r g c", r=2, g=32)
        SAc = self.T_big[:].bitcast(F32)[:, 0:4096].rearrange("p (rg c) -> p rg c", c=CN)
        TS = self.T_c32[:, 0:4096].rearrange("p (b e c) -> p b e c", e=2, c=CN)
        Bsa = self.B_big
        for fc in range(FC):
            sl, Bs = w2sl[fc // 4]
            bq = [self.bank() for _ in range(4)]
            mms = []
            for e in range(2):
                for s in range(8):
                    off = (((fc % 4) * 2 + e) * 8 + s) * 128
                    for q in range(4):
                        mms.append((bq[q][0][:, e * 64:(e + 1) * 64], sl[32 * q:32 * q + 32, off:off + 128],
                                    u8(fc)[32 * q:32 * q + 32, s, :], e == 0 and s == 0, e == 1 and s == 7, (32 * q, 0)))
            self.mm(mms, [Bs, Bu[fc]], [b_[1] for b_ in bq])
            if S5STOP <= 0:
                continue
            for q in range(4):
                self.cp("act" if q % 2 else "dve", TS[:, fc * 4 + q, :, :],
                        bq[q][0][:, 0:128].rearrange("p (e c) -> p e c", e=2), [bq[q][1]], [self.B_c32])
        if S5STOP > 0:
            for e in range(2):
                for ri in range(2):
                    self.cp("act" if (e + ri) % 2 else "dve", SA[64 * e:64 * e + 64, ri, :, :],
                            TS[64 * ri:64 * ri + 64, :, e, :], [self.B_c32], [Bsa])
        if DBG and l == 0:
            fw.dma(fw.sp, self.dbg[:, 0:4096], self.T_a32[:], reads=[self.B_a32], writes=[self.B_dbg], owner=self.B_dbg)
            fw.dma(fw.sp, self.dbg[:, 4096:8192], self.T_big[:].bitcast(F32)[:, 0:4096], reads=[Bsa], writes=[self.B_dbg], owner=self.B_dbg)
        if S5STOP <= 1:
            return
        Hist = self.T_c32[:, 0:(CN + 1) * 96].rearrange("p (c w) -> p c w", w=96)
        Bh = self.B_c32
        A1, A2, BAj = self.A1[j], self.A2[j], self.B_A[j]
        Bht = self.B_htail
        self.cp("pool", Hist[:, 0, :], self.scar[j][:], [self.B_scar[j]], [Bh, Bht])
        t1, Bt1 = self.tmp[0][:, 0:64], self.B_tmp[0]
        t2, Bt2 = self.tmp[1][:, 0:64], self.B_tmp[1]
        for c in range(CN):
            self.tt(REC_ENG, t1, Hist[:, c, 0:64], A1[:], ALU.mult, [Bh, BAj], [Bt1])
            self.tt(REC_ENG, t2, Hist[:, c, 32:96], A2[:], ALU.mult, [Bh, Bht, BAj], [Bt2])
            self.tt(REC_ENG, t1, t1, SAc[:, :, c], ALU.add, [Bt1, Bsa], [Bt1])
            self.tt(REC_ENG, Hist[:, c + 1, 0:64], t1, t2, ALU.add, [Bt1, Bt2], [Bh])
            self.tt(REC_ENG, Hist[:, c + 1, 64:96], t1[:, 0:32], t2[:, 0:32], ALU.add, [Bt1, Bt2], [Bht])
        self.cp("pool", self.scar[j][:], Hist[:, CN, :], [Bh, Bht], [self.B_scar[j]])
        if DBG == 1 and l == 0:
            fw.dma(fw.sp, self.dbg[:, 8192:8192 + 6240], self.T_c32[:, 0:6240], reads=[Bh], writes=[self.B_dbg], owner=self.B_dbg)
        if S5STOP <= 2:
            return
        Hp = self.T_e16[:].rearrange("p (c gg par) -> p c gg par", gg=32, par=2)
        Bhp = self.B_e16
        for par in range(2):
            for ri in range(2):
                eng = "act" if (par + ri) % 2 else "dve"
                self.cp(eng, Hp[64 * ri:64 * ri + 64, :, :, par],
                        Hist[64 * par:64 * par + 64, 0:CN, ri * 32:(ri + 1) * 32], [Bh], [Bhp])
        Hpc = self.T_e16[:].rearrange("p (c g) -> p c g", g=64)
        if S5STOP <= 3:
            return
        w1sl = []
        for hf in range(2):
            w1sl.append(self.wslot([(lambda s: s[:], self.w1s[j][:, hf * 8192:(hf + 1) * 8192], self.B_w1s[j])]))
        w4sl, Bw4 = self.wslot([(lambda s: s[:], self.w4s[j], self.B_w4s[j])])
        Yt = self.T_d32[:].bitcast(BF16).rearrange("p (t g k) -> p t g k", t=8, k=16)
        Byt = self.B_d32_all
        for fcp in range(4):
            bq = [self.bank() for _ in range(4)]
            mms = []
            for q in range(4):
                for fi in range(2):
                    for e in range(2):
                        g = (fcp * 2 + fi) * 8 + 2 * q + e
                        o = bq[q][0][0:CN, (fi * 2 + e) * 128:(fi * 2 + e + 1) * 128]
                        mms.append((o, Hpc[:, :, g], w4sl[:, g * 128:(g + 1) * 128], fi == 0 and e == 0, False, None))
            for fi in range(2):
                fc = fcp * 2 + fi
                sl, Bs = w1sl[fc // 4]
                for e in range(2):
                    for s in range(8):
                        off = (((fc % 4) * 2 + e) * 8 + s) * 128
                        for q in range(4):
                            o = bq[q][0][0:CN, (fi * 2 + e) * 128:(fi * 2 + e + 1) * 128]
                            mms.append((o, u8(fc)[32 * q:32 * q + 32, s, :], sl[32 * q:32 * q + 32, off:off + 128],
                                        False, fi == 1 and e == 1 and s == 7, (32 * q, 0)))
            self.mm(mms, [w1sl[0][1], w1sl[1][1], Bw4, Bu, Bhp], [b_[1] for b_ in bq])
            k = 0
            for q in range(4):
                for fi in range(2):
                    k += 1
                    g0 = (fcp * 2 + fi) * 8 + 2 * q
                    self.cp("act" if q % 2 else "dve", Yt[0:CN, :, g0:g0 + 2, :],
                            bq[q][0][0:CN, fi * 256:(fi + 1) * 256].rearrange("p (g t k) -> p t g k", g=2, t=8), [bq[q][1]], Byt)
        if S5STOP <= 4:
            return
        y2 = self.T_c32[:, 0:4096].rearrange("p (f t) -> p f t", f=FC)
        x32 = self.T_a32[:].rearrange("p (f t) -> p f t", f=FC)
        for fc in range(FC):
            bk, Bb = self.bank()
            bkb = bk[:].bitcast(BF16)
            self.tr([(bkb[:, t * CN:(t + 1) * CN], Yt[0:CN, t, fc * 8:(fc + 1) * 8, :], self.identb[0:CN, 0:CN]) for t in range(8)],
                    Byt + [self.B_idb], [Bb])
            self.stt(y2[:, fc, :].rearrange("p (c t) -> p t c", t=8),
                     x32[:, fc, :].rearrange("p (c t) -> p t c", t=8),
                     self.vecT[:, 104 + j * 8 + fc:104 + j * 8 + fc + 1],
                     bkb[:, 0:TT].rearrange("p (t c) -> p t c", t=8), ALU.mult, ALU.add,
                     [self.B_a32, self.B_vecT, Bb], [self.B_c32])
        if DBG == 2 and l == 0:
            fw.dma(fw.sp, self.dbg[:, 8192:12288], self.T_c32[:, 0:4096], reads=[self.B_c32], writes=[self.B_dbg], owner=self.B_dbg)
        if S5STOP <= 5:
            return
        yf = self.T_c32[:, 0:4096]
        z = self.T_e16[:].rearrange("p (f t) -> p f t", f=FC)
        self.actf(self.T_e16[:], yf, AF.Gelu_apprx_tanh, [self.B_c32], [self.B_e16])
        slv, Bsv = self.wslot([self.wblock(self.glu_b[j], self.B_glu[j], FC, 0, D, 0)])
        slg, Bsgl = self.wslot([self.wblock(self.glu_b[j], self.B_glu[j], FC, D, D, 0)])
        for m in range(FC):
            bv, Bbv = self.proj8(slv, Bsv, 0, D, m, z, self.B_e16)
            bg, Bbg = self.proj8(slg, Bsgl, 0, D, m, z, self.B_e16)
            sg, Bsg = self.next_sg()
            self.actf(sg, bg[:], AF.Sigmoid, [Bbg], [Bsg])
            tm, Btm = self.next_tmp()
            self.tt("dve", tm, sg, bv[:], ALU.mult, [Bsg, Bbv], [Btm])
            self.resid_add(m, tm, Btm)

    def load_x(self, tile):
        fw = self.fw
        tok0 = tile * TT
        xt = self.T_a32[:].rearrange("p (j f) -> p j f", j=4)
        fw.dma(fw.sp, xt, self.x[tok0:tok0 + TT, :].rearrange("(j q) f -> q j f", q=128), writes=[self.B_a32])
        for fc in range(FC):
            bk, Bb = self.bank()
            self.tr([(bk[:, jj * 128:(jj + 1) * 128], xt[:, jj, fc * 128:(fc + 1) * 128], self.ident[:]) for jj in range(4)],
                    [self.B_a32, self.B_id], [Bb])
            self.cp("act" if fc % 2 else "dve", self.T_h[:, fc, :], bk[:], [Bb], [self.B_h[fc]])
            self.h_written(fc)

    def store_out(self, tile):
        fw = self.fw
        tok0 = tile * TT
        self.rmsnorm(96, want32=True)
        y = self.T_a32[:].rearrange("p (f t) -> p f t", f=FC)
        ot = self.T_c32[:, 0:4096].rearrange("p (j f) -> p j f", j=4)
        for jj in range(4):
            for half in range(2):
                bk, Bb = self.bank()
                self.tr([(bk[:, k * 128:(k + 1) * 128], y[:, half * 4 + k, jj * 128:(jj + 1) * 128], self.ident[:]) for k in range(4)],
                        [self.B_a32, self.B_id], [Bb])
                self.cp("act" if half else "dve", ot[:, jj, half * 512:(half + 1) * 512], bk[:], [Bb], [self.B_c32])
        fw.dma(fw.pool, self.out[tok0:tok0 + TT, :].rearrange("(j q) f -> q j f", q=128), ot,
               reads=[self.B_c32], writes=[self.B_out], owner=self.B_out)

    def build(self, stage=99):
        docast = stage >= 1 and not int(os.environ.get('NOCAST', '0'))
        self.consts()
        self.fence()
        if stage >= 2:
            self.s5_prologue(0)
            if docast:
                self.cast_weights([0], gate=self.B_mark[0])
            self.fence()
            self.s5_prologue(1)
            if docast:
                self.cast_weights([1, 2, 3], gate=self.B_mark[1])
            self.fence()
        elif docast:
            self.cast_weights([0, 1, 2, 3])
        if stage >= 3:
            nsub = min(stage - 3, 3 * DEPTH)
            for tile in range(self.ntiles):
                self.load_x(tile)
                for k in range(nsub):
                    l, kind = divmod(k, 3)
                    j = l // 2
                    if kind == 0:
                        if l % 2 == 0:
                            self.s5(j, l)
                        else:
                            self.conv(j, l)
                    elif kind == 1:
                        self.ffn(l)
                    else:
                        self.ple(l, tile)
                self.store_out(tile)
        allb = ([self.B_out, self.B_dbg] + self.B_glu + self.B_cwi + self.B_cwo + self.B_fin + self.B_fout + self.B_pg + self.B_pu
                + self.B_w1s + self.B_w2s + self.B_w4s)
        self.fw.wait_all(self.fw.pool, allb)
        self.fw.wait_all(self.fw.sp, allb)


def build_nc(ntiles=SEQ // TT, stage=99):
    nc = bass.Bass("TRN2", target_bir_lowering=False)
    with contextlib.ExitStack() as st:
        Prog(nc, st, ntiles).build(stage)
    return nc


def make_in_maps(inputs, n_cores=8):
    f = lambda a: np.ascontiguousarray(np.asarray(a, dtype=np.float32))
    vecs = np.concatenate([
        f(inputs["norm_mix_g"]).reshape(32, 128), f(inputs["norm_ffn_g"]).reshape(32, 128),
        f(inputs["norm_ple_g"]).reshape(32, 128), f(inputs["final_norm_g"]).reshape(8, 128),
        f(inputs["s5_d"]).reshape(16, 128), f(inputs["conv_w"]).reshape(48, 128)], axis=0)
    cst = np.zeros((128, 8), np.float32)
    par = (np.arange(128) // 16) % 2
    cst[:, 0] = (par == 0)
    cst[:, 1] = (par == 1)
    shared = {"vecs": vecs, "cst": cst}
    for k in ("s5_a_re", "s5_a_im", "s5_log_dt", "s5_b_re", "s5_b_im", "s5_c_re", "s5_c_im", "s5_w_glu",
              "conv_w_in", "conv_w_out", "ffn_w_in", "ffn_w_out", "ple_w_gate", "ple_w_up"):
        shared[k] = f(inputs[k])
    x = f(inputs["x"])
    p = f(inputs["p"])
    maps = []
    for c in range(n_cores):
        m = dict(shared)
        m["x"] = x[c]
        m["p"] = np.ascontiguousarray(p[:, c])
        maps.append(m)
    return maps


def kernel(**inputs):
    nc = build_nc()
    in_maps = make_in_maps(inputs)
    res = run_bass_kernel_spmd(nc, in_maps, core_ids=list(range(8)))
    return np.stack([np.asarray(r["out"], dtype=np.float32) for r in res.results], axis=0)
```

```python
import contextlib
import math
import numpy as np
import concourse.bass as bass
import concourse.mybir as mybir
from concourse.bass_utils import run_bass_kernel_spmd

F32 = mybir.dt.float32
BF16 = mybir.dt.bfloat16
I32 = mybir.dt.int32
AF = mybir.ActivationFunctionType
ALU = mybir.AluOpType

D = 1024
FC = 8
SEQ = 4096
TT = 512
CN = TT // 8
HID = 2816
HC = 22
PLE = 256
DEPTH = 4
EPS = 1e-6
NSLOT = 5
import os
S5STOP = int(os.environ.get('S5STOP', '99'))
DBG = int(os.environ.get('KDBG', '0'))
REC_ENG = os.environ.get('REC_ENG', 'dve')
TWO_PI = float(2 * np.pi)


class Buf:
    __slots__ = ("name", "w", "r", "dsem", "dcnt")

    def __init__(self, name):
        self.name = name
        self.w = None
        self.r = {}
        self.dsem = None
        self.dcnt = 0


class Eng:
    def __init__(self, fw, name, e):
        self.name = name
        self.e = e
        self.sem = fw.new_sem("s_" + name)
        self.cnt = 0
        self.seen = {}


class FW:
    SEM_ROLL = 30000

    def __init__(self, nc, stack):
        self.nc = nc
        self.stack = stack
        self.nsem = 0
        self.pe = Eng(self, "pe", nc.tensor)
        self.act = Eng(self, "act", nc.scalar)
        self.dve = Eng(self, "dve", nc.vector)
        self.pool = Eng(self, "pool", nc.gpsimd)
        self.sp = Eng(self, "sp", nc.sync)

    def new_sem(self, name):
        self.nsem += 1
        return self.stack.enter_context(self.nc.semaphore("%s_%d" % (name, self.nsem)))

    def sbuf(self, name, shape, dtype):
        return self.stack.enter_context(self.nc.sbuf_tensor(name, shape, dtype))

    def psum(self, name, shape, dtype):
        return self.stack.enter_context(self.nc.psum_tensor(name, shape, dtype))

    @staticmethod
    def _flat(bs):
        out = []
        for b in bs:
            if isinstance(b, (list, tuple)):
                out.extend(FW._flat(b))
            else:
                out.append(b)
        return out

    def _deps(self, reads, writes):
        deps = {}

        def add(ev):
            if ev is None:
                return
            k = id(ev[0])
            if k not in deps or deps[k][1] < ev[1]:
                deps[k] = ev
        for b in reads:
            add(b.w)
        for b in writes:
            add(b.w)
            for ev in b.r.values():
                add(ev)
        return deps

    def _wait(self, eng, deps):
        for k, (s, v) in deps.items():
            if eng.seen.get(k, 0) < v:
                eng.e.wait_ge(s, v)
                eng.seen[k] = v

    def _record(self, ev, reads, writes):
        k = id(ev[0])
        for b in reads:
            if k not in b.r or b.r[k][1] < ev[1]:
                b.r[k] = ev
        for b in writes:
            b.w = ev
            b.r = {}

    def op(self, eng, emit, reads=(), writes=()):
        reads = self._flat(reads)
        writes = self._flat(writes)
        self._wait(eng, self._deps(reads, writes))
        ins = emit(eng.e)
        if eng.cnt >= self.SEM_ROLL:
            eng.sem = self.new_sem("s_" + eng.name)
            eng.cnt = 0
        eng.cnt += 1
        ins.then_inc(eng.sem, 1)
        ev = (eng.sem, eng.cnt)
        self._record(ev, reads, writes)
        return ev

    def dma(self, eng, out, in_, reads=(), writes=(), owner=None, **kw):
        reads = self._flat(reads)
        writes = self._flat(writes)
        self._wait(eng, self._deps(reads, writes))
        if owner is None:
            owner = writes[0] if writes else reads[0]
        if owner.dsem is None:
            owner.dsem = self.new_sem("d_" + owner.name)
        ins = eng.e.dma_start(out=out, in_=in_, **kw)
        owner.dcnt += 16
        ins.then_inc(owner.dsem, 16)
        ev = (owner.dsem, owner.dcnt)
        self._record(ev, reads, writes)
        return ev

    def wait_all(self, eng, bufs):
        deps = self._deps(bufs, bufs)
        self._wait(eng, deps)


class Prog:
    def __init__(self, nc, st, ntiles):
        self.nc = nc
        self.fw = FW(nc, st)
        self.ntiles = ntiles
        fw = self.fw
        dt = nc.dram_tensor
        self.x = dt("x", [SEQ, D], F32, kind="ExternalInput").ap()
        self.p = dt("p", [DEPTH, SEQ, PLE], F32, kind="ExternalInput").ap()
        self.vecs = dt("vecs", [168, 128], F32, kind="ExternalInput").ap()
        self.cst = dt("cst", [128, 8], F32, kind="ExternalInput").ap()
        self.s5_a_re = dt("s5_a_re", [2, 64, 64], F32, kind="ExternalInput").ap()
        self.s5_a_im = dt("s5_a_im", [2, 64, 64], F32, kind="ExternalInput").ap()
        self.s5_log_dt = dt("s5_log_dt", [2, 64], F32, kind="ExternalInput").ap()
        self.s5_b_re = dt("s5_b_re", [2, 64, 64, 16], F32, kind="ExternalInput").ap()
        self.s5_b_im = dt("s5_b_im", [2, 64, 64, 16], F32, kind="ExternalInput").ap()
        self.s5_c_re = dt("s5_c_re", [2, 64, 16, 64], F32, kind="ExternalInput").ap()
        self.s5_c_im = dt("s5_c_im", [2, 64, 16, 64], F32, kind="ExternalInput").ap()
        self.w_glu = dt("s5_w_glu", [2, D, 2 * D], F32, kind="ExternalInput").ap()
        self.cw_in = dt("conv_w_in", [2, D, 3 * D], F32, kind="ExternalInput").ap()
        self.cw_out = dt("conv_w_out", [2, D, D], F32, kind="ExternalInput").ap()
        self.f_in = dt("ffn_w_in", [DEPTH, D, 2 * HID], F32, kind="ExternalInput").ap()
        self.f_out = dt("ffn_w_out", [DEPTH, HID, D], F32, kind="ExternalInput").ap()
        self.pg = dt("ple_w_gate", [DEPTH, D, D], F32, kind="ExternalInput").ap()
        self.pu = dt("ple_w_up", [DEPTH, PLE, D], F32, kind="ExternalInput").ap()
        self.out = dt("out", [SEQ, D], F32, kind="ExternalOutput").ap()
        self.dbg = dt("dbg", [128, 16384], F32, kind="ExternalOutput").ap() if DBG else None
        self.B_dbg = Buf("dbg")
        self.glu_b = [dt("glu_b%d" % j, [D, 2 * D], BF16, kind="Internal").ap() for j in range(2)]
        self.cwi_b = [dt("cwi_b%d" % j, [D, 3 * D], BF16, kind="Internal").ap() for j in range(2)]
        self.cwo_b = [dt("cwo_b%d" % j, [D, D], BF16, kind="Internal").ap() for j in range(2)]
        self.fin_b = [dt("fin_b%d" % l, [D, 2 * HID], BF16, kind="Internal").ap() for l in range(DEPTH)]
        self.fout_b = [dt("fout_b%d" % l, [HID, D], BF16, kind="Internal").ap() for l in range(DEPTH)]
        self.pg_b = [dt("pg_b%d" % l, [D, D], BF16, kind="Internal").ap() for l in range(DEPTH)]
        self.pu_b = [dt("pu_b%d" % l, [PLE, D], BF16, kind="Internal").ap() for l in range(DEPTH)]
        self.w1s = [dt("w1s%d" % j, [128, 16384], BF16, kind="Internal").ap() for j in range(2)]
        self.w2s = [dt("w2s%d" % j, [128, 16384], BF16, kind="Internal").ap() for j in range(2)]
        self.w4s = [dt("w4s%d" % j, [128, 8192], BF16, kind="Internal").ap() for j in range(2)]
        B = Buf
        self.B_glu = [B("glu%d" % j) for j in range(2)]
        self.B_cwi = [B("cwi%d" % j) for j in range(2)]
        self.B_cwo = [B("cwo%d" % j) for j in range(2)]
        self.B_fin = [B("fin%d" % l) for l in range(DEPTH)]
        self.B_fout = [B("fout%d" % l) for l in range(DEPTH)]
        self.B_pg = [B("pg%d" % l) for l in range(DEPTH)]
        self.B_pu = [B("pu%d" % l) for l in range(DEPTH)]
        self.B_w1s = [B("w1s%d" % j) for j in range(2)]
        self.B_w2s = [B("w2s%d" % j) for j in range(2)]
        self.B_w4s = [B("w4s%d" % j) for j in range(2)]
        self.B_out = B("out")
        sb = fw.sbuf
        self.ring = [sb("ring%d" % i, [128, 8192], BF16) for i in range(NSLOT)]
        self.B_ring = [B("ring%d" % i) for i in range(NSLOT)]
        self.ring_i = 0
        self.T_h = sb("T_h", [128, FC, TT], F32); self.B_h = [B("h%d" % i) for i in range(FC)]
        self.T_xn = sb("T_xn", [128, FC, TT], BF16); self.B_xn = [B("xn%d" % i) for i in range(FC)]
        self.T_sq = sb("T_sq", [128, FC, TT], BF16); self.B_sq = [B("sq%d" % i) for i in range(FC)]
        self.pending = []
        self.stat_n = 0
        self.T_a32 = sb("T_a32", [128, 4096], F32); self.B_a32 = B("a32")
        self.T_big = sb("T_big", [128, 11264], BF16); self.B_big = B("big")
        self.T_c32 = sb("T_c32", [128, 6272], F32); self.B_c32 = B("c32")
        self.T_d32 = sb("T_d32", [128, 4096], F32)
        self.T_e16 = sb("T_e16", [128, 4096], BF16); self.B_e16 = B("e16")
        self.T_rstd = sb("T_rstd", [128, TT], F32); self.B_rstd = B("rstd")
        self.vecT = sb("vecT", [128, 168], F32); self.B_vecT = B("vecT")
        self.cstT = sb("cstT", [128, 8], F32); self.B_cst = B("cst")
        self.ident = sb("ident", [128, 128], F32); self.B_id = B("ident")
        self.identb = sb("identb", [128, 128], BF16); self.B_idb = B("identb")
        self.ones = sb("ones", [128, 128], BF16); self.B_ones = B("ones")
        self.ccar = [sb("ccar%d" % j, [128, FC, 2], F32) for j in range(2)]
        self.B_ccar = [B("ccar%d" % j) for j in range(2)]
        self.scar = [sb("scar%d" % j, [128, 96], F32) for j in range(2)]
        self.B_scar = [B("scar%d" % j) for j in range(2)]
        self.A1 = [sb("A1_%d" % j, [128, 64], F32) for j in range(2)]
        self.A2 = [sb("A2_%d" % j, [128, 64], F32) for j in range(2)]
        self.B_A = [B("A%d" % j) for j in range(2)]
        d = self.T_d32
        self.sg = [d[:, 0:512], d[:, 512:1024]]
        self.B_sg = [B("sg0"), B("sg1")]
        self.sgi = 0
        self.tmp = [d[:, 1024:1536], d[:, 1536:2048]]
        self.B_tmp = [B("tmp0"), B("tmp1")]
        self.tmpi = 0
        self.cvm = [d[:, 2048:2562], d[:, 2562:3076]]
        self.B_cvm = [B("cvm0"), B("cvm1")]
        self.pt = d[:, 3076:4096]
        self.B_d32_all = self.B_sg + self.B_tmp + self.B_cvm
        self.pT = sb("pT", [128, 2, TT], BF16); self.B_pT = B("pT")
        self.fz = sb("fz", [128, 8], F32)
        self.B_mark = [None, None]
        self.B_htail = Buf("htail")
        self.banks = [fw.psum("bank%d" % i, [128, 512], F32) for i in range(8)]
        self.B_banks = [B("bank%d" % i) for i in range(8)]
        self.bank_i = 0

    def bank(self):
        i = self.bank_i
        self.bank_i = (i + 1) % 7
        return self.banks[i], self.B_banks[i]

    def h_written(self, m):
        self.actf(self.T_sq[:, m, :], self.T_h[:, m, :], AF.Square, [self.B_h[m]], [self.B_sq[m]])
        self.pending.append(m)
        self.flush_stats(2)

    def flush_stats(self, keep):
        while len(self.pending) > keep:
            m = self.pending.pop(0)
            n = self.stat_n
            self.mm([(self.banks[7][:], self.ones[:], self.T_sq[:, m, :], n == 0, n == FC - 1, None)],
                    [self.B_ones, self.B_sq[m]], [self.B_banks[7]])
            self.stat_n = (n + 1) % FC

    def slot(self):
        i = self.ring_i
        self.ring_i = (i + 1) % NSLOT
        return self.ring[i], self.B_ring[i]

    def next_sg(self):
        i = self.sgi
        self.sgi ^= 1
        return self.sg[i], self.B_sg[i]

    def next_tmp(self):
        i = self.tmpi
        self.tmpi ^= 1
        return self.tmp[i], self.B_tmp[i]

    def mm(self, mms, reads, writes):
        def emit(e):
            ins = None
            for (o, l, r, st, sp, tp) in mms:
                if tp is None:
                    ins = e.matmul(o, l, r, start=st, stop=sp)
                else:
                    ins = e.matmul(o, l, r, start=st, stop=sp, tile_position=tp)
            return ins
        return self.fw.op(self.fw.pe, emit, reads, writes)

    def tr(self, trs, reads, writes):
        def emit(e):
            ins = None
            for (o, i, idn) in trs:
                ins = e.transpose(o, i, idn)
            return ins
        return self.fw.op(self.fw.pe, emit, reads, writes)

    def E(self, name):
        return getattr(self.fw, name)

    def tt(self, eng, out, in0, in1, op, reads, writes):
        return self.fw.op(self.E(eng), lambda e: e.tensor_tensor(out=out, in0=in0, in1=in1, op=op), reads, writes)

    def ts(self, eng, out, in0, s1, op0, reads, writes, s2=None, op1=None):
        if op1 is None:
            return self.fw.op(self.E(eng), lambda e: e.tensor_scalar(out=out, in0=in0, scalar1=s1, scalar2=None, op0=op0), reads, writes)
        return self.fw.op(self.E(eng), lambda e: e.tensor_scalar(out=out, in0=in0, scalar1=s1, scalar2=s2, op0=op0, op1=op1), reads, writes)

    def stt(self, out, in0, scalar, in1, op0, op1, reads, writes):
        return self.fw.op(self.fw.dve, lambda e: e.scalar_tensor_tensor(out=out, in0=in0, scalar=scalar, in1=in1, op0=op0, op1=op1), reads, writes)

    def actf(self, out, in_, func, reads, writes, scale=1.0, bias=0.0):
        return self.fw.op(self.fw.act, lambda e: e.activation(out=out, in_=in_, func=func, scale=scale, bias=bias), reads, writes)

    def cp(self, eng, out, in_, reads, writes):
        if eng == "act":
            return self.fw.op(self.fw.act, lambda e: e.copy(out=out, in_=in_), reads, writes)
        return self.fw.op(self.E(eng), lambda e: e.tensor_copy(out=out, in_=in_), reads, writes)

    def memset(self, eng, ap, val, writes):
        return self.fw.op(self.E(eng), lambda e: e.memset(ap, val), (), writes)

    def fence(self):
        bufs = ([self.B_a32, self.B_big, self.B_c32, self.B_e16, self.B_rstd, self.B_xn, self.B_h]
                + self.B_d32_all + self.B_ring + self.B_banks + [self.B_htail])
        self.fw.op(self.fw.dve, lambda e: e.memset(self.fz[:], 0.0), bufs, bufs)

    def cast_weights(self, layers, gate=None):
        fw = self.fw
        first = [True]

        def cast(dst, src, B, rows):
            for r0 in range(0, rows, 128):
                rd = [gate] if (gate is not None and first[0]) else []
                first[0] = False
                fw.dma(fw.pool, dst[r0:r0 + 128, :], src[r0:r0 + 128, :], reads=rd, writes=[B], owner=B)
        order = []
        for l in layers:
            j = l // 2
            if l % 2 == 0:
                order.append((self.glu_b[j], self.w_glu[j], self.B_glu[j], D))
            else:
                order.append((self.cwi_b[j], self.cw_in[j], self.B_cwi[j], D))
                order.append((self.cwo_b[j], self.cw_out[j], self.B_cwo[j], D))
            order.append((self.fin_b[l], self.f_in[l], self.B_fin[l], D))
            order.append((self.fout_b[l], self.f_out[l], self.B_fout[l], HID))
            order.append((self.pg_b[l], self.pg[l], self.B_pg[l], D))
            order.append((self.pu_b[l], self.pu[l], self.B_pu[l], PLE))
        for a in order:
            cast(*a)

    def consts(self):
        fw = self.fw
        self.memset("pool", self.ident[:], 0.0, [self.B_id])
        fw.op(fw.pool, lambda e: e.affine_select(out=self.ident[:], in_=self.ident[:], pattern=[[-1, 128]],
                                                  compare_op=ALU.not_equal, fill=1.0, base=0, channel_multiplier=1),
              [self.B_id], [self.B_id])
        self.cp("dve", self.identb[:], self.ident[:], [self.B_id], [self.B_idb])
        self.memset("dve", self.ones[:], 1.0, [self.B_ones])
        fw.dma(fw.sp, self.cstT[:], self.cst, writes=[self.B_cst])
        for j in range(2):
            self.memset("dve", self.ccar[j][:], 0.0, [self.B_ccar[j]])
            self.memset("dve", self.scar[j][:], 0.0, [self.B_scar[j]])
        v = self.T_a32
        fw.dma(fw.sp, v[:, 0:128], self.vecs[0:128, :], writes=[self.B_a32])
        fw.dma(fw.sp, v[0:40, 128:256], self.vecs[128:168, :], writes=[self.B_a32])
        bk, Bb = self.bank()
        self.tr([(bk[:, 0:128], v[:, 0:128], self.ident[:]),
                 (bk[:, 128:168], v[0:40, 128:256], self.ident[0:40, 0:40])],
                [self.B_a32, self.B_id], [Bb])
        self.cp("dve", self.vecT[:], bk[:, 0:168], [Bb], [self.B_vecT])

    def s5_prologue(self, j):
        fw = self.fw
        nc = self.nc
        c32 = self.T_c32
        Bc = self.B_c32
        R = lambda i: c32[:, i * 512:(i + 1) * 512]
        d32 = self.T_d32
        Bd = self.B_sg[0]
        Pt = lambda i: d32[:, 2048 + i * 64: 2048 + (i + 1) * 64]
        cT_re = d32[:, 0:1024]
        cT_im = d32[:, 1024:2048]
        a32 = self.T_a32
        Ba = self.B_a32
        big32 = self.T_big[:].bitcast(F32)
        Bg = self.B_big

        def G(tau):
            if tau < 4:
                return a32[:, tau * 1024:(tau + 1) * 1024]
            return big32[:, (tau - 4) * 1024:(tau - 3) * 1024]

        def GB(tau):
            return Ba if tau < 4 else Bg
        slA, BA = self.slot()
        slB, BB = self.slot()
        slC, BC = self.slot()
        slD, BD = self.slot()
        slE, BE = self.slot()
        C32 = slC[:].bitcast(F32)
        D32 = slD[:].bitcast(F32)
        b_re_P = C32[:, 0:1024]
        b_im_P = C32[:, 1024:2048]
        Bst = C32[:, 2048:3072]
        Cin = C32[:, 3072:4096]
        Bst_pad = D32[:, 0:2048]
        KoutS = D32[:, 2048:4096]

        Ain = R(0)
        for (src, c0) in ((self.s5_a_re[j], 0), (self.s5_a_im[j], 128)):
            fw.dma(fw.sp, Ain[0:64, c0:c0 + 64], src, writes=[Bc])
            fw.dma(fw.sp, Ain[0:64, c0 + 64:c0 + 128], src, writes=[Bc])
        bk, Bb = self.bank()
        self.tr([(bk[:, 0:64], Ain[0:64, 0:128], self.ident[0:64, 0:64]),
                 (bk[:, 64:128], Ain[0:64, 128:256], self.ident[0:64, 0:64])], [Bc, self.B_id], [Bb])
        are_P, aim_P, ldt_P = Pt(0), Pt(1), Pt(2)
        self.cp("dve", d32[:, 2048:2176], bk[:, 0:128], [Bb], [Bd])
        fw.dma(fw.sp, ldt_P, self.s5_log_dt[j].partition_broadcast(128), writes=[Bd])
        are_R, aim_R = R(1), R(2)
        ldt_R8 = Pt(3)[:, 0:8]
        for gl in range(8):
            for (src, dst) in ((self.s5_a_re[j], are_R), (self.s5_a_im[j], aim_R)):
                fw.dma(fw.sp, dst[16 * gl:16 * gl + 16, :].rearrange("k (fc p) -> k fc p", fc=8),
                       src.rearrange("(fc gl) p -> gl fc p", gl=8)[gl].partition_broadcast(16), writes=[Bc])
            fw.dma(fw.sp, ldt_R8[16 * gl:16 * gl + 16, :],
                   self.s5_log_dt[j].rearrange("(fc gl) -> gl fc", gl=8)[gl].partition_broadcast(16), writes=[Bd],
                   allow_slow_non_contiguous=True)
        for (src, dst) in ((self.s5_b_re[j], b_re_P), (self.s5_b_im[j], b_im_P)):
            sv = src.rearrange("g p k -> p g k")
            dv = dst.rearrange("p (g k) -> p g k", k=16)
            for half in range(2):
                for g0 in range(0, 64, 16):
                    fw.dma(fw.sp, dv[64 * half:64 * half + 64, g0:g0 + 16, :], sv[:, g0:g0 + 16, :], writes=[BC])
        b_re_R, b_im_R = R(3), R(4)
        for (srcP, dstR) in ((b_re_P, b_re_R), (b_im_P, b_im_R)):
            bk, Bb = self.bank()
            self.tr([(bk[:, fc * 64:(fc + 1) * 64], srcP[0:64, fc * 128:(fc + 1) * 128], self.ident[0:64, 0:64])
                     for fc in range(8)], [BC, self.B_id], [Bb])
            self.cp("dve", dstR, bk[:], [Bb], [Bc])
        for (src, dstT) in ((self.s5_c_re[j], cT_re), (self.s5_c_im[j], cT_im)):
            sv = src.rearrange("g j p -> (g j) p").rearrange("(r q) p -> q r p", q=128)
            cv = Cin.rearrange("q (r c) -> q r c", c=128)
            fw.dma(fw.sp, cv[:, :, 0:64], sv, writes=[BC])
            fw.dma(fw.sp, cv[:, :, 64:128], sv, writes=[BC])
            for hb in range(2):
                bk, Bb = self.bank()
                self.tr([(bk[:, r * 128:(r + 1) * 128], cv[:, hb * 4 + r, :], self.ident[:]) for r in range(4)],
                        [BC, self.B_id], [Bb])
                self.cp("dve", dstT[:, hb * 512:(hb + 1) * 512], bk[:], [Bb], [Bd])

        self.B_mark[j] = Buf("mark%d" % j)
        self.fw.op(self.fw.dve, lambda e: e.memset(self.fz[:, 0:4], 0.0), [Bc, Bd, BC], [self.B_mark[j]])
        def ctab(N, are, aim, dtb, T, B):
            E = "dve"
            x, y, k_i, m1, den = T(4), T(5), T(6), T(7), T(8)
            ab_re, ab_im, f_re, f_im = T(0), T(1), T(2), T(3)
            self.tt(E, x, are, dtb, ALU.mult, [B], [B])
            self.actf(x, x, AF.Exp, [B], [B])
            self.tt(E, y, aim, dtb, ALU.mult, [B], [B])
            self.ts(E, y, y, 1.0 / TWO_PI, ALU.mult, [B], [B])

            def sin_turns(dst, src, shift):
                ki = k_i.bitcast(I32)
                self.ts(E, dst, src, float(shift), ALU.add, [B], [B])
                self.cp(E, ki, dst, [B], [B])
                self.cp(E, m1, ki, [B], [B])
                self.tt(E, dst, dst, m1, ALU.subtract, [B], [B])
                self.ts(E, m1, dst, 0.5, ALU.is_gt, [B], [B])
                self.tt(E, dst, dst, m1, ALU.subtract, [B], [B])
                self.ts(E, m1, dst, -0.5, ALU.is_lt, [B], [B])
                self.tt(E, dst, dst, m1, ALU.add, [B], [B])
                self.actf(dst, dst, AF.Sin, [B], [B], scale=TWO_PI)
            sin_turns(ab_im, y, 0.0)
            sin_turns(ab_re, y, 0.25)
            self.tt(E, ab_re, ab_re, x, ALU.mult, [B], [B])
            self.tt(E, ab_im, ab_im, x, ALU.mult, [B], [B])
            self.tt(E, den, are, are, ALU.mult, [B], [B])
            self.tt(E, m1, aim, aim, ALU.mult, [B], [B])
            self.tt(E, den, den, m1, ALU.add, [B], [B])
            self.fw.op(self.fw.dve, lambda e: e.reciprocal(out=den, in_=den), [B], [B])
            self.ts(E, x, ab_re, -1.0, ALU.add, [B], [B])
            self.tt(E, f_re, x, are, ALU.mult, [B], [B])
            self.tt(E, m1, ab_im, aim, ALU.mult, [B], [B])
            self.tt(E, f_re, f_re, m1, ALU.add, [B], [B])
            self.tt(E, f_re, f_re, den, ALU.mult, [B], [B])
            self.tt(E, f_im, ab_im, are, ALU.mult, [B], [B])
            self.tt(E, m1, x, aim, ALU.mult, [B], [B])
            self.tt(E, f_im, f_im, m1, ALU.subtract, [B], [B])
            self.tt(E, f_im, f_im, den, ALU.mult, [B], [B])
            return ab_re, ab_im, f_re, f_im

        def cmul(o_re, o_im, a_re, a_im, b_re, b_im, t1, t2, B):
            E = "dve"
            self.tt(E, t1, a_re, b_re, ALU.mult, [B], [B])
            self.tt(E, t2, a_im, b_im, ALU.mult, [B], [B])
            self.tt(E, t1, t1, t2, ALU.subtract, [B], [B])
            self.tt(E, t2, a_re, b_im, ALU.mult, [B], [B])
            self.tt(E, o_im, a_im, b_re, ALU.mult, [B], [B])
            self.tt(E, o_im, o_im, t2, ALU.add, [B], [B])
            self.cp(E, o_re, t1, [B], [B])

        dtP = Pt(4)
        self.actf(dtP, ldt_P, AF.Exp, [Bd], [Bd])
        PT = lambda i: Pt(5 + i)
        ab_re, ab_im, f_re, f_im = ctab(64, are_P, aim_P, dtP, PT, Bd)
        U, V = Pt(16), Pt(17)
        self.cp("dve", U[0:64, :], f_re[0:64, :], [Bd], [Bd])
        self.cp("dve", U[64:128, :], f_im[64:128, :], [Bd], [Bd])
        self.ts("dve", V[0:64, :], f_im[0:64, :], -1.0, ALU.mult, [Bd], [Bd])
        self.cp("dve", V[64:128, :], f_re[64:128, :], [Bd], [Bd])
        bc16 = lambda t: t.unsqueeze(2).to_broadcast([128, 64, 16])
        v3 = lambda t: t.rearrange("p (g k) -> p g k", k=16)
        tmpK = KoutS
        self.tt("dve", v3(Bst), v3(b_re_P), bc16(U), ALU.mult, [BC, Bd], [BC])
        self.tt("dve", v3(tmpK[:, 0:1024]), v3(b_im_P), bc16(V), ALU.mult, [BC, Bd], [BD])
        self.tt("dve", Bst, Bst, tmpK[:, 0:1024], ALU.add, [BC, BD], [BC])
        self.memset("dve", Bst_pad, 0.0, [BD])
        bp = Bst_pad.rearrange("p (g e k) -> p g e k", e=2, k=16)
        bs = v3(Bst).rearrange("p (gg par) k -> p gg par k", par=2)
        bpp = bp.rearrange("p (gg par) e k -> p gg par e k", par=2)
        self.cp("dve", bpp[:, :, 0, 0, :], bs[:, :, 0, :], [BC], [BD])
        self.cp("dve", bpp[:, :, 1, 1, :], bs[:, :, 1, :], [BC], [BD])
        Pre, Pim, t1, t2, X, Y = Pt(18), Pt(19), Pt(20), Pt(21), Pt(22), Pt(23)
        self.memset("dve", Pre, 1.0, [Bd])
        self.memset("dve", Pim, 0.0, [Bd])
        g3 = lambda t: t.rearrange("p (g k) -> p g k", k=16)
        for tau in range(9):
            if tau > 0:
                cmul(Pre, Pim, Pre, Pim, ab_re, ab_im, t1, t2, Bd)
            self.cp("dve", X[0:64, :], Pre[0:64, :], [Bd], [Bd])
            self.ts("dve", X[64:128, :], Pim[64:128, :], -1.0, ALU.mult, [Bd], [Bd])
            self.cp("dve", Y[0:64, :], Pim[0:64, :], [Bd], [Bd])
            self.cp("dve", Y[64:128, :], Pre[64:128, :], [Bd], [Bd])
            Gt = G(tau)
            BG = GB(tau)
            self.tt("dve", g3(Gt), g3(cT_re), bc16(X), ALU.mult, [Bd], [BG])
            self.tt("dve", g3(tmpK[:, 0:1024]), g3(cT_im), bc16(Y), ALU.mult, [Bd], [BD])
            self.tt("dve", Gt, Gt, tmpK[:, 0:1024], ALU.subtract, [BG, BD], [BG])
        A1, A2 = self.A1[j], self.A2[j]
        BAj = self.B_A[j]
        for half in range(2):
            ps = slice(64 * half, 64 * half + 64)
            src_re = Pre.rearrange("p (gg par) -> p gg par", par=2)[ps, :, half]
            src_im = Pim.rearrange("p (gg par) -> p gg par", par=2)[ps, :, half]
            self.cp("dve", A1[ps, 0:32], src_re, [Bd], [BAj])
            self.cp("dve", A1[ps, 32:64], src_re, [Bd], [BAj])
            self.ts("dve", A2[ps, 0:32], src_im, -1.0, ALU.mult, [Bd], [BAj])
            self.cp("dve", A2[ps, 32:64], src_im, [Bd], [BAj])
        w4v = slE[:].rearrange("p (g t k) -> p g t k", t=8, k=16)
        for t in range(8):
            self.cp("act", w4v[:, :, t, :], g3(G(t + 1)), [GB(t + 1)], [BE])
        fw.dma(fw.sp, self.w4s[j], slE[:], reads=[BE], writes=[self.B_w4s[j]], owner=self.B_w4s[j])
        kb = []
        for i in range(4):
            kb.append(self.bank())
        for g in range(64):
            fc, gl = divmod(g, 8)
            q, e = divmod(gl, 2)
            col = (fc * 2 + e) * 128
            bkk, Bk = kb[col // 512]
            c0 = col % 512
            mms = []
            for half in range(2):
                base = a32 if half == 0 else big32
                rhs = base[:, 0:4096].rearrange("p (tau g k) -> p tau g k", tau=4, k=16)[:, :, g, :]
                mms.append((bkk[32 * q:32 * q + 32, c0 + half * 64:c0 + half * 64 + 64],
                            Bst_pad[:, g * 32:(g + 1) * 32], rhs, True, True, (0, 32 * q)))
            self.mm(mms, [BD, Ba, Bg], [Bk])
        for i in range(4):
            self.cp("act", KoutS[:, i * 512:(i + 1) * 512], kb[i][0][:], [kb[i][1]], [BD])
        dtR8 = Pt(24)[:, 0:8]
        self.actf(dtR8, ldt_R8, AF.Exp, [Bd], [Bd])
        dtR = R(5)
        self.cp("dve", dtR.rearrange("p (fc q) -> p fc q", fc=8), dtR8.unsqueeze(2).to_broadcast([128, 8, 64]), [Bd], [Bc])
        sc = [R(6), R(7), R(8), R(9), R(10), R(11), R(0)]
        extra = self.T_e16[:].bitcast(F32)
        Be = self.B_e16
        sc += [extra[:, i * 512:(i + 1) * 512] for i in range(4)]
        Bboth = None

        class _T:
            pass
        RT = lambda i: sc[i]
        def ctabR():
            saved_tt, saved_ts, saved_cp, saved_actf = self.tt, self.ts, self.cp, self.actf
            return ctab(512, are_R, aim_R, dtR, RT, Bc)
        self.memset("dve", extra[:, 0:2048], 0.0, [Be, Bc])
        abr, abi, fr, fi = ctabR()
        Qre, Qim, q1, q2 = sc[9], sc[10], sc[4], sc[5]
        self.cp("dve", Qre, fr, [Bc], [Bc])
        self.cp("dve", Qim, fi, [Bc], [Bc])
        wre, wim = sc[6], sc[7]
        w2A = slA[:].rearrange("p (fc e s c) -> p fc e s c", fc=4, e=2, s=8)
        w2B = slB[:].rearrange("p (fc e s c) -> p fc e s c", fc=4, e=2, s=8)
        for m in range(8):
            s = 7 - m
            if m > 0:
                cmul(Qre, Qim, Qre, Qim, abr, abi, q1, q2, Bc)
            self.tt("dve", wre, Qre, b_re_R, ALU.mult, [Bc], [Bc])
            self.tt("dve", q1, Qim, b_im_R, ALU.mult, [Bc], [Bc])
            self.tt("dve", wre, wre, q1, ALU.subtract, [Bc], [Bc])
            self.tt("dve", wim, Qim, b_re_R, ALU.mult, [Bc], [Bc])
            self.tt("dve", q1, Qre, b_im_R, ALU.mult, [Bc], [Bc])
            self.tt("dve", wim, wim, q1, ALU.add, [Bc], [Bc])
            for e in range(2):
                msk = self.cstT[:, e:e + 1]
                for (w2, Bw, f0) in ((w2A, BA, 0), (w2B, BB, 4)):
                    for (src, c0) in ((wre, 0), (wim, 64)):
                        self.ts("dve", w2[:, :, e, s, c0:c0 + 64],
                                src.rearrange("p (fc q) -> p fc q", fc=8)[:, f0:f0 + 4, :], msk, ALU.mult,
                                [Bc, self.B_cst], [Bw])
        fw.dma(fw.sp, self.w2s[j][:, 0:8192], slA[:], reads=[BA], writes=[self.B_w2s[j]], owner=self.B_w2s[j])
        fw.dma(fw.sp, self.w2s[j][:, 8192:16384], slB[:], reads=[BB], writes=[self.B_w2s[j]], owner=self.B_w2s[j])
        self.memset("dve", slA[:], 0.0, [BA])
        self.memset("dve", slB[:], 0.0, [BB])
        kv = KoutS.rearrange("p (fe tau k) -> p fe tau k", tau=8, k=16)
        for (sl, Bw, f0) in ((slA, BA, 0), (slB, BB, 8)):
            wv = sl[:].rearrange("p (fe s t k) -> p fe s t k", s=8, t=8, k=16)
            for s in range(8):
                self.cp("act" if s % 2 else "dve", wv[:, :, s, s:8, :], kv[:, f0:f0 + 8, 0:8 - s, :], [BD], [Bw])
        fw.dma(fw.sp, self.w1s[j][:, 0:8192], slA[:], reads=[BA], writes=[self.B_w1s[j]], owner=self.B_w1s[j])
        fw.dma(fw.sp, self.w1s[j][:, 8192:16384], slB[:], reads=[BB], writes=[self.B_w1s[j]], owner=self.B_w1s[j])

    def wslot(self, pieces):
        sl, Bs = self.slot()
        for (dstf, src, Bsrc) in pieces:
            self.fw.dma(self.fw.sp, dstf(sl), src, reads=[Bsrc], writes=[Bs])
        return sl, Bs

    def wblock(self, src, Bsrc, kc, c0, w, off=0):
        return (lambda sl: sl[:, off:off + kc * w].rearrange("p (kc m) -> p kc m", kc=kc),
                src.rearrange("(kc p) m -> p kc m", p=128)[:, :, c0:c0 + w], Bsrc)

    def rmsnorm(self, col, want32=False):
        h = self.T_h
        self.flush_stats(0)
        assert self.stat_n == 0
        bk, Bb = self.banks[7], self.B_banks[7]
        self.actf(self.T_rstd[:], bk[:], AF.Sqrt, [Bb], [self.B_rstd], scale=1.0 / D, bias=EPS)
        self.fw.op(self.fw.dve, lambda e: e.reciprocal(out=self.T_rstd[:], in_=self.T_rstd[:]), [self.B_rstd], [self.B_rstd])
        if want32:
            x32 = self.T_a32[:].rearrange("p (f t) -> p f t", f=FC)
            for fc in range(FC):
                self.stt(x32[:, fc, :], h[:, fc, :], self.vecT[:, col + fc:col + fc + 1], self.T_rstd[:],
                         ALU.mult, ALU.mult, [self.B_h[fc], self.B_rstd, self.B_vecT], [self.B_a32])
                self.cp("act", self.T_xn[:, fc, :], x32[:, fc, :], [self.B_a32], [self.B_xn[fc]])
        else:
            for fc in range(FC):
                self.stt(self.T_xn[:, fc, :], h[:, fc, :], self.vecT[:, col + fc:col + fc + 1], self.T_rstd[:],
                         ALU.mult, ALU.mult, [self.B_h[fc], self.B_rstd, self.B_vecT], [self.B_xn[fc]])

    def proj8(self, sl, Bs, off, w, m, rhs, Brhs, kc=FC, split=False):
        bk, Bb = self.bank()
        mms = [(bk[:], sl[:, off + k * w + m * 128: off + k * w + (m + 1) * 128], rhs[:, k, :], k == 0, k == kc - 1, None)
               for k in range(kc)]
        if split:
            for k in range(kc):
                wr = [Bb] if (k == 0 or k == kc - 1) else []
                self.mm([mms[k]], [Bs, Brhs[k]], wr)
        else:
            self.mm(mms, [Bs, Brhs], [Bb])
        return bk, Bb

    def resid_add(self, m, src, Bsrc):
        self.tt("dve", self.T_h[:, m, :], self.T_h[:, m, :], src, ALU.add, [self.B_h[m], Bsrc], [self.B_h[m]])
        self.h_written(m)

    def ffn(self, l):
        self.rmsnorm(32 + l * 8)
        hid = self.T_big[:].rearrange("p (j t) -> p j t", t=TT)
        for (j0, nj) in ((0, 4), (4, 4), (8, 4), (12, 4), (16, 4), (20, 2)):
            w = nj * 128
            sl, Bs = self.wslot([self.wblock(self.fin_b[l], self.B_fin[l], FC, j0 * 128, w, 0),
                                 self.wblock(self.fin_b[l], self.B_fin[l], FC, HID + j0 * 128, w, 4096)])
            for jj in range(nj):
                bg, Bg = self.proj8(sl, Bs, 0, w, jj, self.T_xn, self.B_xn, split=(j0 + jj == 0))
                bu, Bu = self.proj8(sl, Bs, 4096, w, jj, self.T_xn, self.B_xn)
                sg, Bsg = self.next_sg()
                self.actf(sg, bg[:], AF.Silu, [Bg], [Bsg])
                self.tt("dve", hid[:, j0 + jj, :], sg, bu[:], ALU.mult, [Bsg, Bu], [self.B_big])
        for mp in range(4):
            sl, Bs = self.wslot([(lambda s: s[:, 0:HC * 256].rearrange("p (hc m) -> p hc m", hc=HC),
                                  self.fout_b[l].rearrange("(hc p) m -> p hc m", p=128)[:, :, mp * 256:(mp + 1) * 256],
                                  self.B_fout[l])])
            for mi in range(2):
                bk, Bb = self.bank()
                self.mm([(bk[:], sl[:, hc * 256 + mi * 128: hc * 256 + (mi + 1) * 128], hid[:, hc, :], hc == 0, hc == HC - 1, None)
                         for hc in range(HC)], [Bs, self.B_big], [Bb])
                self.resid_add(mp * 2 + mi, bk[:], Bb)

    def ple(self, l, tile):
        fw = self.fw
        tok0 = tile * TT
        pt = self.T_c32[:, 0:1024].rearrange("p (j f) -> p j f", j=4)
        fw.dma(fw.sp, pt, self.p[l, tok0:tok0 + TT, :].rearrange("(j q) f -> q j f", q=128), writes=[self.B_c32])
        for kc in range(2):
            bk, Bb = self.bank()
            self.tr([(bk[:, jj * 128:(jj + 1) * 128], pt[:, jj, kc * 128:(kc + 1) * 128], self.ident[:]) for jj in range(4)],
                    [self.B_c32, self.B_id], [Bb])
            self.cp("act", self.pT[:, kc, :], bk[:], [Bb], [self.B_pT])
        self.rmsnorm(64 + l * 8)
        slg, Bsg_ = self.wslot([self.wblock(self.pg_b[l], self.B_pg[l], FC, 0, D, 0)])
        slu, Bsu = self.wslot([self.wblock(self.pu_b[l], self.B_pu[l], 2, 0, D, 0)])
        for m in range(FC):
            bg, Bg = self.proj8(slg, Bsg_, 0, D, m, self.T_xn, self.B_xn, split=(m == 0))
            bu, Bu = self.proj8(slu, Bsu, 0, D, m, self.pT, self.B_pT, kc=2)
            sg, Bsg = self.next_sg()
            self.actf(sg, bg[:], AF.Sigmoid, [Bg], [Bsg])
            tm, Btm = self.next_tmp()
            self.tt("dve", tm, sg, bu[:], ALU.mult, [Bsg, Bu], [Btm])
            self.resid_add(m, tm, Btm)

    def conv(self, j, l):
        self.rmsnorm(l * 8)
        slc, Bsc = self.wslot([self.wblock(self.cwi_b[j], self.B_cwi[j], FC, D, D, 0)])
        slv, Bsv = self.wslot([self.wblock(self.cwi_b[j], self.B_cwi[j], FC, 2 * D, D, 0)])
        slb, Bsb = self.wslot([self.wblock(self.cwi_b[j], self.B_cwi[j], FC, 0, D, 0)])
        acc = self.T_c32[:, 0:512]
        bcv = self.T_e16[:].rearrange("p (f t) -> p f t", f=FC)
        wcol = lambda i, m: self.vecT[:, 120 + (j * 3 + i) * 8 + m: 120 + (j * 3 + i) * 8 + m + 1]
        for m in range(FC):
            bc_, Bbc = self.proj8(slc, Bsc, 0, D, m, self.T_xn, self.B_xn, split=(m == 0))
            bv, Bbv = self.proj8(slv, Bsv, 0, D, m, self.T_xn, self.B_xn)
            bb, Bbb = self.proj8(slb, Bsb, 0, D, m, self.T_xn, self.B_xn)
            sg, Bsg = self.next_sg()
            self.cp("act", sg, bc_[:], [Bbc], [Bsg])
            cvm, Bcv = self.cvm[m % 2], self.B_cvm[m % 2]
            self.cp("pool", cvm[:, 0:2], self.ccar[j][:, m, :], [self.B_ccar[j]], [Bcv])
            self.tt("dve", cvm[:, 2:2 + TT], sg, bv[:], ALU.mult, [Bsg, Bbv], [Bcv])
            self.cp("pool", self.ccar[j][:, m, :], cvm[:, TT:TT + 2], [Bcv], [self.B_ccar[j]])
            self.ts("dve", acc, cvm[:, 2:2 + TT], wcol(2, m), ALU.mult, [Bcv, self.B_vecT], [self.B_c32])
            self.stt(acc, cvm[:, 1:1 + TT], wcol(1, m), acc, ALU.mult, ALU.add, [Bcv, self.B_vecT, self.B_c32], [self.B_c32])
            self.stt(acc, cvm[:, 0:TT], wcol(0, m), acc, ALU.mult, ALU.add, [Bcv, self.B_vecT, self.B_c32], [self.B_c32])
            self.tt("dve", bcv[:, m, :], acc, bb[:], ALU.mult, [self.B_c32, Bbb], [self.B_e16])
        slo, Bso = self.wslot([self.wblock(self.cwo_b[j], self.B_cwo[j], FC, 0, D, 0)])
        for m in range(FC):
            bk, Bb = self.proj8(slo, Bso, 0, D, m, bcv, self.B_e16)
            self.resid_add(m, bk[:], Bb)

    def s5(self, j, l):
        fw = self.fw
        self.rmsnorm(l * 8, want32=True)
        u = self.T_xn
        Bu = self.B_xn
        u8 = lambda fc: u[:, fc, :].rearrange("p (c s) -> p s c", s=8)
        w2sl = []
        for hf in range(2):
            w2sl.append(self.wslot([(lambda s: s[:], self.w2s[j][:, hf * 8192:(hf + 1) * 8192], self.B_w2s[j])]))
        SA = self.T_big[:].bitcast(F32)[:, 0:4096].rearrange("p (r g c) -> p r g c", r=2, g=32)
        SAc = self.T_big[:].bitcast(F32)[:, 0:4096].rearrange("p (rg c) -> p rg c", c=CN)
        TS = self.T_c32[:, 0:4096].rearrange("p (b e c) -> p b e c", e=2, c=CN)
        Bsa = self.B_big
        for fc in range(FC):
            sl, Bs = w2sl[fc // 4]
            bq = [self.bank() for _ in range(4)]
            mms = []
            for e in range(2):
                for s in range(8):
                    off = (((fc % 4) * 2 + e) * 8 + s) * 128
                    for q in range(4):
                        mms.append((bq[q][0][:, e * 64:(e + 1) * 64], sl[32 * q:32 * q + 32, off:off + 128],
                                    u8(fc)[32 * q:32 * q + 32, s, :], e == 0 and s == 0, e == 1 and s == 7, (32 * q, 0)))
            self.mm(mms, [Bs, Bu[fc]], [b_[1] for b_ in bq])
            if S5STOP <= 0:
                continue
            for q in range(4):
                self.cp("act" if q % 2 else "dve", TS[:, fc * 4 + q, :, :],
                        bq[q][0][:, 0:128].rearrange("p (e c) -> p e c", e=2), [bq[q][1]], [self.B_c32])
        if S5STOP > 0:
            for e in range(2):
                for ri in range(2):
                    self.cp("act" if (e + ri) % 2 else "dve", SA[64 * e:64 * e + 64, ri, :, :],
                            TS[64 * ri:64 * ri + 64, :, e, :], [self.B_c32], [Bsa])
        if DBG and l == 0:
            fw.dma(fw.sp, self.dbg[:, 0:4096], self.T_a32[:], reads=[self.B_a32], writes=[self.B_dbg], owner=self.B_dbg)
            fw.dma(fw.sp, self.dbg[:, 4096:8192], self.T_big[:].bitcast(F32)[:, 0:4096], reads=[Bsa], writes=[self.B_dbg], owner=self.B_dbg)
        if S5STOP <= 1:
            return
        w1sl = []
        for hf in range(2):
            w1sl.append(self.wslot([(lambda s: s[:], self.w1s[j][:, hf * 8192:(hf + 1) * 8192], self.B_w1s[j])]))

        def y_w1(fcp, bq):
            mms = []
            for fi in range(2):
                fc = fcp * 2 + fi
                sl, Bs = w1sl[fc // 4]
                for e in range(2):
                    for s in range(8):
                        off = (((fc % 4) * 2 + e) * 8 + s) * 128
                        for q in range(4):
                            o = bq[q][0][0:CN, (fi * 2 + e) * 128:(fi * 2 + e + 1) * 128]
                            mms.append((o, u8(fc)[32 * q:32 * q + 32, s, :], sl[32 * q:32 * q + 32, off:off + 128],
                                        fi == 0 and e == 0 and s == 0, False, (32 * q, 0)))
            self.mm(mms, [w1sl[0][1], w1sl[1][1], Bu], [b_[1] for b_ in bq])
        bq0 = [self.bank() for _ in range(4)]
        y_w1(0, bq0)
        Hist = self.T_c32[:, 0:(CN + 1) * 96].rearrange("p (c w) -> p c w", w=96)
        Bh = self.B_c32
        A1, A2, BAj = self.A1[j], self.A2[j], self.B_A[j]
        Bht = self.B_htail
        self.cp("pool", Hist[:, 0, :], self.scar[j][:], [self.B_scar[j]], [Bh, Bht])
        t1, Bt1 = self.tmp[0][:, 0:64], self.B_tmp[0]
        t2, Bt2 = self.tmp[1][:, 0:64], self.B_tmp[1]
        for c in range(CN):
            self.tt(REC_ENG, t1, Hist[:, c, 0:64], A1[:], ALU.mult, [Bh, BAj], [Bt1])
            self.tt(REC_ENG, t2, Hist[:, c, 32:96], A2[:], ALU.mult, [Bh, Bht, BAj], [Bt2])
            self.tt(REC_ENG, t1, t1, SAc[:, :, c], ALU.add, [Bt1, Bsa], [Bt1])
            self.tt(REC_ENG, Hist[:, c + 1, 0:64], t1, t2, ALU.add, [Bt1, Bt2], [Bh])
            self.tt(REC_ENG, Hist[:, c + 1, 64:96], t1[:, 0:32], t2[:, 0:32], ALU.add, [Bt1, Bt2], [Bht])
        self.cp("pool", self.scar[j][:], Hist[:, CN, :], [Bh, Bht], [self.B_scar[j]])
        if DBG == 1 and l == 0:
            fw.dma(fw.sp, self.dbg[:, 8192:8192 + 6240], self.T_c32[:, 0:6240], reads=[Bh], writes=[self.B_dbg], owner=self.B_dbg)
        if S5STOP <= 2:
            return
        Hp = self.T_e16[:].rearrange("p (c gg par) -> p c gg par", gg=32, par=2)
        Bhp = self.B_e16
        for par in range(2):
            for ri in range(2):
                eng = "act" if (par + ri) % 2 else "dve"
                self.cp(eng, Hp[64 * ri:64 * ri + 64, :, :, par],
                        Hist[64 * par:64 * par + 64, 0:CN, ri * 32:(ri + 1) * 32], [Bh], [Bhp])
        Hpc = self.T_e16[:].rearrange("p (c g) -> p c g", g=64)
        if S5STOP <= 3:
            return
        w4sl, Bw4 = self.wslot([(lambda s: s[:], self.w4s[j], self.B_w4s[j])])
        Yt = self.T_d32[:].bitcast(BF16).rearrange("p (t g k) -> p t g k", t=8, k=16)
        Byt = self.B_d32_all
        for fcp in range(4):
            if fcp == 0:
                bq = bq0
            else:
                bq = [self.bank() for _ in range(4)]
                y_w1(fcp, bq)
            mms = []
            for q in range(4):
                for fi in range(2):
                    for e in range(2):
                        g = (fcp * 2 + fi) * 8 + 2 * q + e
                        o = bq[q][0][0:CN, (fi * 2 + e) * 128:(fi * 2 + e + 1) * 128]
                        mms.append((o, Hpc[:, :, g], w4sl[:, g * 128:(g + 1) * 128], False, fi == 1 and e == 1, None))
            self.mm(mms, [Bw4, Bhp], [b_[1] for b_ in bq])
            k = 0
            for q in range(4):
                for fi in range(2):
                    k += 1
                    g0 = (fcp * 2 + fi) * 8 + 2 * q
                    self.cp("act" if q % 2 else "dve", Yt[0:CN, :, g0:g0 + 2, :],
                            bq[q][0][0:CN, fi * 256:(fi + 1) * 256].rearrange("p (g t k) -> p t g k", g=2, t=8), [bq[q][1]], Byt)
        if S5STOP <= 4:
            return
        y2 = self.T_c32[:, 0:4096].rearrange("p (f t) -> p f t", f=FC)
        x32 = self.T_a32[:].rearrange("p (f t) -> p f t", f=FC)
        for fc in range(FC):
            bk, Bb = self.bank()
            bkb = bk[:].bitcast(BF16)
            self.tr([(bkb[:, t * CN:(t + 1) * CN], Yt[0:CN, t, fc * 8:(fc + 1) * 8, :], self.identb[0:CN, 0:CN]) for t in range(8)],
                    Byt + [self.B_idb], [Bb])
            self.stt(y2[:, fc, :].rearrange("p (c t) -> p t c", t=8),
                     x32[:, fc, :].rearrange("p (c t) -> p t c", t=8),
                     self.vecT[:, 104 + j * 8 + fc:104 + j * 8 + fc + 1],
                     bkb[:, 0:TT].rearrange("p (t c) -> p t c", t=8), ALU.mult, ALU.add,
                     [self.B_a32, self.B_vecT, Bb], [self.B_c32])
        if DBG == 2 and l == 0:
            fw.dma(fw.sp, self.dbg[:, 8192:12288], self.T_c32[:, 0:4096], reads=[self.B_c32], writes=[self.B_dbg], owner=self.B_dbg)
        if S5STOP <= 5:
            return
        yf = self.T_c32[:, 0:4096]
        z = self.T_e16[:].rearrange("p (f t) -> p f t", f=FC)
        self.actf(self.T_e16[:], yf, AF.Gelu_apprx_tanh, [self.B_c32], [self.B_e16])
        slv, Bsv = self.wslot([self.wblock(self.glu_b[j], self.B_glu[j], FC, 0, D, 0)])
        slg, Bsgl = self.wslot([self.wblock(self.glu_b[j], self.B_glu[j], FC, D, D, 0)])
        for m in range(FC):
            bv, Bbv = self.proj8(slv, Bsv, 0, D, m, z, self.B_e16)
            bg, Bbg = self.proj8(slg, Bsgl, 0, D, m, z, self.B_e16)
            sg, Bsg = self.next_sg()
            self.actf(sg, bg[:], AF.Sigmoid, [Bbg], [Bsg])
            tm, Btm = self.next_tmp()
            self.tt("dve", tm, sg, bv[:], ALU.mult, [Bsg, Bbv], [Btm])
            self.resid_add(m, tm, Btm)

    def prefetch_x(self, tile):
        tok0 = tile * TT
        xt = self.T_big[:].bitcast(F32)[:, 0:4096].rearrange("p (j f) -> p j f", j=4)
        self.fw.dma(self.fw.sp, xt, self.x[tok0:tok0 + TT, :].rearrange("(j q) f -> q j f", q=128), writes=[self.B_big])

    def load_x(self, tile):
        xt = self.T_big[:].bitcast(F32)[:, 0:4096].rearrange("p (j f) -> p j f", j=4)
        for fc in range(FC):
            bk, Bb = self.bank()
            self.tr([(bk[:, jj * 128:(jj + 1) * 128], xt[:, jj, fc * 128:(fc + 1) * 128], self.ident[:]) for jj in range(4)],
                    [self.B_big, self.B_id], [Bb])
            self.cp("act" if fc % 2 else "dve", self.T_h[:, fc, :], bk[:], [Bb], [self.B_h[fc]])
            self.h_written(fc)

    def store_out(self, tile):
        fw = self.fw
        tok0 = tile * TT
        self.rmsnorm(96, want32=True)
        y = self.T_a32[:].rearrange("p (f t) -> p f t", f=FC)
        ot = self.T_c32[:, 0:4096].rearrange("p (j f) -> p j f", j=4)
        for jj in range(4):
            for half in range(2):
                bk, Bb = self.bank()
                self.tr([(bk[:, k * 128:(k + 1) * 128], y[:, half * 4 + k, jj * 128:(jj + 1) * 128], self.ident[:]) for k in range(4)],
                        [self.B_a32, self.B_id], [Bb])
                self.cp("act" if half else "dve", ot[:, jj, half * 512:(half + 1) * 512], bk[:], [Bb], [self.B_c32])
        fw.dma(fw.pool, self.out[tok0:tok0 + TT, :].rearrange("(j q) f -> q j f", q=128), ot,
               reads=[self.B_c32], writes=[self.B_out], owner=self.B_out)

    def build(self, stage=99):
        docast = stage >= 1 and not int(os.environ.get('NOCAST', '0'))
        self.consts()
        self.fence()
        if stage >= 2:
            self.s5_prologue(0)
            if docast:
                self.cast_weights([0], gate=self.B_mark[0])
            self.fence()
            self.s5_prologue(1)
            if docast:
                self.cast_weights([1, 2, 3], gate=self.B_mark[1])
            self.fence()
        elif docast:
            self.cast_weights([0, 1, 2, 3])
        if stage >= 3:
            nsub = min(stage - 3, 3 * DEPTH)
            self.prefetch_x(0)
            for tile in range(self.ntiles):
                self.load_x(tile)
                for k in range(nsub):
                    l, kind = divmod(k, 3)
                    j = l // 2
                    if kind == 0:
                        if l % 2 == 0:
                            self.s5(j, l)
                        else:
                            self.conv(j, l)
                    elif kind == 1:
                        self.ffn(l)
                        if l == DEPTH - 1 and tile + 1 < self.ntiles:
                            self.prefetch_x(tile + 1)
                    else:
                        self.ple(l, tile)
                self.store_out(tile)
        allb = ([self.B_out, self.B_dbg] + self.B_glu + self.B_cwi + self.B_cwo + self.B_fin + self.B_fout + self.B_pg + self.B_pu
                + self.B_w1s + self.B_w2s + self.B_w4s)
        self.fw.wait_all(self.fw.pool, allb)
        self.fw.wait_all(self.fw.sp, allb)


def build_nc(ntiles=SEQ // TT, stage=99):
    nc = bass.Bass("TRN2", target_bir_lowering=False)
    with contextlib.ExitStack() as st:
        Prog(nc, st, ntiles).build(stage)
    return nc


def make_in_maps(inputs, n_cores=8):
    f = lambda a: np.ascontiguousarray(np.asarray(a, dtype=np.float32))
    vecs = np.concatenate([
        f(inputs["norm_mix_g"]).reshape(32, 128), f(inputs["norm_ffn_g"]).reshape(32, 128),
        f(inputs["norm_ple_g"]).reshape(32, 128), f(inputs["final_norm_g"]).reshape(8, 128),
        f(inputs["s5_d"]).reshape(16, 128), f(inputs["conv_w"]).reshape(48, 128)], axis=0)
    cst = np.zeros((128, 8), np.float32)
    par = (np.arange(128) // 16) % 2
    cst[:, 0] = (par == 0)
    cst[:, 1] = (par == 1)
    shared = {"vecs": vecs, "cst": cst}
    for k in ("s5_a_re", "s5_a_im", "s5_log_dt", "s5_b_re", "s5_b_im", "s5_c_re", "s5_c_im", "s5_w_glu",
              "conv_w_in", "conv_w_out", "ffn_w_in", "ffn_w_out", "ple_w_gate", "ple_w_up"):
        shared[k] = f(inputs[k])
    x = f(inputs["x"])
    p = f(inputs["p"])
    maps = []
    for c in range(n_cores):
        m = dict(shared)
        m["x"] = x[c]
        m["p"] = np.ascontiguousarray(p[:, c])
        maps.append(m)
    return maps


def kernel(**inputs):
    nc = build_nc()
    in_maps = make_in_maps(inputs)
    res = run_bass_kernel_spmd(nc, in_maps, core_ids=list(range(8)))
    return np.stack([np.asarray(r["out"], dtype=np.float32) for r in res.results], axis=0)
```
